# Optimizing a Trainium2 kernel written in Bass

```python
import jax, jax.numpy as jnp
from jax import lax
import numpy as np

D_MODEL = 1024
BATCH = 8
SEQ = 2048
DEPTH = 1
DEC_BATCH = 128
DEC_SEQ = 8
PAST_LEN = 16384
PAGE_SIZE = 128

D_MIX = D_MODEL
D_A = D_MIX // 2
D_B = D_MIX - D_A
HEAD_A = 128
H_A = D_A // HEAD_A
DK = HEAD_A
DV = HEAD_A
HEAD_B = 128
H_B = D_B // HEAD_B
CHUNK_A = 64
CHUNK_B = 128
D_IN = 4 * D_A + 3 * D_B
EPS = 1e-6

kernel_name = "hymba_hgrn2_chunkmlp_step"


def _rms_norm(x, g):
    xf = x.astype(jnp.float32)
    y = xf * lax.rsqrt(jnp.mean(xf * xf, axis=-1, keepdims=True) + EPS)
    return (y * g.astype(jnp.float32)).astype(x.dtype)


def _layer_norm(x, g, b):
    xf = x.astype(jnp.float32)
    mu = jnp.mean(xf, axis=-1, keepdims=True)
    var = jnp.mean(jnp.square(xf - mu), axis=-1, keepdims=True)
    y = (xf - mu) * lax.rsqrt(var + EPS)
    return (y * g.astype(jnp.float32) + b.astype(jnp.float32)).astype(x.dtype)


def _hgrn2_chunked(q, logf, k, v, s0):
    B, L, H, _ = q.shape
    C = min(CHUNK_A, L)
    n_chunks = -(-L // C)
    pad = n_chunks * C - L

    def to_chunks(a):
        a = jnp.pad(a, ((0, 0), (0, pad), (0, 0), (0, 0)))
        return a.reshape(B, n_chunks, C, H, a.shape[-1]).transpose(1, 0, 2, 3, 4)

    mask = jnp.tril(jnp.ones((C, C), dtype=bool))[None, :, :, None, None]

    def step(S, inp):
        qc, lfc, kc, vc = inp
        b = jnp.cumsum(lfc, axis=1)
        o_inter = jnp.einsum('bthk,bhkv->bthv', qc * jnp.exp(b), S)
        diff = b[:, :, None] - b[:, None, :]
        decay = jnp.where(mask, jnp.exp(jnp.where(mask, diff, 0.0)), 0.0)
        scores = jnp.einsum('bthk,btshk,bshk->bhts', qc, decay, kc)
        o_intra = jnp.einsum('bhts,bshv->bthv', scores, vc)
        b_last = b[:, -1]
        S_new = jnp.exp(b_last)[..., None] * S + jnp.einsum(
            'bshk,bshv->bhkv', kc * jnp.exp(b_last[:, None] - b), vc)
        return S_new, o_inter + o_intra

    s_final, o = lax.scan(step, s0, (to_chunks(q), to_chunks(logf), to_chunks(k), to_chunks(v)))
    o = o.transpose(1, 0, 2, 3, 4).reshape(B, n_chunks * C, H, v.shape[-1])[:, :L]
    return o, s_final


def _chunk_spatial_mix(v, w_s, b_s):
    B, L, H, E = v.shape
    n_chunks = -(-L // CHUNK_B)
    pad = n_chunks * CHUNK_B - L
    vc = jnp.pad(v, ((0, 0), (0, pad), (0, 0), (0, 0))).reshape(B, n_chunks, CHUNK_B, H, E)
    w_causal = jnp.where(jnp.tril(jnp.ones((CHUNK_B, CHUNK_B), dtype=bool))[None], w_s, 0.0)
    out = jnp.einsum('hts,bnshe->bnthe', w_causal, vc) + b_s.T[None, None, :, :, None]
    return out.reshape(B, n_chunks * CHUNK_B, H, E)[:, :L]


def _mixer_layer(x, s0, lb, norm_g, w_in, hgrn_norm_g, sgu_ln_g, sgu_ln_b, w_s, b_s, w_out):
    B, L, _ = x.shape
    h = _rms_norm(x, norm_g)
    proj = jnp.einsum('bld,de->ble', h, w_in)
    splits = [D_A, 2 * D_A, 3 * D_A, 4 * D_A, 4 * D_A + D_B, 4 * D_A + 2 * D_B]
    q, f, i, z_a, u, v, z_b = jnp.split(proj, splits, axis=-1)

    qf = jax.nn.silu(q.astype(jnp.float32)).reshape(B, L, H_A, DK)
    lbr = lb.reshape(H_A, DK)
    forget = lbr + (1.0 - lbr) * jax.nn.sigmoid(f.astype(jnp.float32)).reshape(B, L, H_A, DK)
    logf = jnp.log(forget)
    k = 1.0 - forget
    vi = i.astype(jnp.float32).reshape(B, L, H_A, DV)
    o_a, s_new = _hgrn2_chunked(qf, logf, k, vi, s0.astype(jnp.float32))
    o_a = _rms_norm(o_a, hgrn_norm_g.reshape(H_A, DV)).reshape(B, L, D_A)
    o_a = o_a * jax.nn.silu(z_a.astype(jnp.float32))

    u = jax.nn.gelu(u)
    v_n = _layer_norm(jax.nn.gelu(v), sgu_ln_g, sgu_ln_b).reshape(B, L, H_B, HEAD_B)
    sgu = u.reshape(B, L, H_B, HEAD_B) * _chunk_spatial_mix(v_n, w_s, b_s)
    o_b = sgu.reshape(B, L, D_B) * jax.nn.silu(z_b)

    mixed = jnp.concatenate([o_a.astype(x.dtype), o_b.astype(x.dtype)], axis=-1)
    out = x + jnp.einsum('ble,ed->bld', mixed, w_out)
    n_open = L - ((L - 1) // CHUNK_B) * CHUNK_B
    v_open = v_n[:, L - n_open:]
    return out, s_new.astype(s0.dtype), v_open


def setup_inputs(seed: int = 0) -> dict:
    key = jax.random.key(seed)
    ks = jax.random.split(key, 14)
    f32 = jnp.float32
    x_prompt = jax.random.normal(ks[0], (BATCH, SEQ, D_MODEL), f32)
    x_sample = jax.random.normal(ks[1], (DEC_BATCH, DEC_SEQ, D_MODEL), f32)
    state_hgrn = 0.3 * jax.random.normal(ks[2], (DEPTH, DEC_BATCH, H_A, DK, DV), f32)
    norm_g = 1.0 + 0.02 * jax.random.normal(ks[3], (DEPTH, D_MODEL), f32)
    w_in = jax.random.normal(ks[4], (DEPTH, D_MODEL, D_IN), f32) * D_MODEL ** -0.5
    lb_logits = 0.5 * jax.random.normal(ks[5], (DEPTH + 1, D_A), f32)
    hgrn_norm_g = 1.0 + 0.02 * jax.random.normal(ks[6], (DEPTH, D_A), f32)
    sgu_ln_g = 1.0 + 0.02 * jax.random.normal(ks[7], (DEPTH, D_B), f32)
    sgu_ln_b = 0.02 * jax.random.normal(ks[8], (DEPTH, D_B), f32)
    w_s = jax.random.normal(ks[9], (DEPTH, H_B, CHUNK_B, CHUNK_B), f32) * CHUNK_B ** -0.5
    b_s = 1.0 + 0.1 * jax.random.normal(ks[10], (DEPTH, H_B, CHUNK_B), f32)
    w_out = jax.random.normal(ks[11], (DEPTH, D_MIX, D_MODEL), f32) * D_MIX ** -0.5
    final_norm_g = 1.0 + 0.02 * jax.random.normal(ks[12], (D_MODEL,), f32)
    return {"x_prompt": x_prompt, "x_sample": x_sample, "state_hgrn": state_hgrn,
            "norm_g": norm_g, "w_in": w_in, "lb_logits": lb_logits,
            "hgrn_norm_g": hgrn_norm_g, "sgu_ln_g": sgu_ln_g, "sgu_ln_b": sgu_ln_b,
            "w_s": w_s, "b_s": b_s, "w_out": w_out, "final_norm_g": final_norm_g}


def reference(x_prompt, x_sample, state_hgrn, norm_g, w_in, lb_logits, hgrn_norm_g,
              sgu_ln_g, sgu_ln_b, w_s, b_s, w_out, final_norm_g):
    lower_bounds = jnp.cumsum(jax.nn.softmax(lb_logits.astype(jnp.float32), axis=0), axis=0)
    B = x_prompt.shape[0]
    xp, xs = x_prompt, x_sample
    sp_list, ss_list, vp_list, vs_list = [], [], [], []
    for l in range(DEPTH):
        lw = (lower_bounds[l], norm_g[l], w_in[l], hgrn_norm_g[l], sgu_ln_g[l], sgu_ln_b[l],
              w_s[l], b_s[l], w_out[l])
        s0_p = jnp.zeros((B, H_A, DK, DV), state_hgrn.dtype)
        xp, sp, vp = _mixer_layer(xp, s0_p, *lw)
        xs, ss, vs = _mixer_layer(xs, state_hgrn[l], *lw)
        sp_list.append(sp); ss_list.append(ss); vp_list.append(vp); vs_list.append(vs)
    y_prompt = _rms_norm(xp, final_norm_g)
    y_sample = _rms_norm(xs, final_norm_g)
    state_hgrn_prompt = jnp.stack(sp_list, axis=0)
    state_hgrn_sample = jnp.stack(ss_list, axis=0)
    v_chunk_prompt = jnp.stack(vp_list, axis=0)
    v_chunk_sample = jnp.stack(vs_list, axis=0)
    return (y_prompt, y_sample, state_hgrn_prompt, state_hgrn_sample, v_chunk_prompt, v_chunk_sample)
```

```python
import numpy as np
import ml_dtypes
from contextlib import ExitStack
import concourse.bass as bass
import concourse.mybir as mybir
from concourse.bass_utils import run_bass_kernel_spmd

F32 = mybir.dt.float32
BF16 = mybir.dt.bfloat16
U8 = mybir.dt.uint8
AF = mybir.ActivationFunctionType
ALU = mybir.AluOpType

EPS = 1e-6
NCORES = 8
SEQ = 2048
NTILE_P = SEQ // 128
NMT_P = NTILE_P // 2
D = 1024
DIN = 3584
G_Q, G_F, G_I, G_ZA, G_U, G_V, G_ZB = range(7)


class Buf:
    __slots__ = ("name", "last_w", "readers")

    def __init__(self, name):
        self.name = name
        self.last_w = None
        self.readers = []


class Sched:
    ENGS = ("pe", "act", "dve", "pool", "sp")

    def __init__(self, nc, es):
        self.nc = nc
        self.es = es
        self.lists = {e: [] for e in self.ENGS}
        self.sems = {}
        self.count = {}
        self.seen = {e: {} for e in self.ENGS}
        for e in ("pe", "act", "dve", "pool"):
            self._mksem("eng_" + e)

    def _mksem(self, key):
        self.sems[key] = self.es.enter_context(self.nc.semaphore(key))
        self.count[key] = 0
        return key

    def dmasem(self, key):
        return self._mksem("dma_" + key)

    def _deps(self, eng, reads, writes):
        need = {}

        def add(ev):
            if ev is None:
                return
            k, v = ev
            if need.get(k, 0) < v:
                need[k] = v
        own = "eng_" + eng
        for b in reads:
            add(b.last_w)
        for b in writes:
            add(b.last_w)
            for r in b.readers:
                add(r)
        out = []
        for k, v in need.items():
            if self.seen[eng].get(k, 0) >= v:
                continue
            self.seen[eng][k] = v
            out.append((k, v))
        return out

    def _record(self, ev, reads, writes):
        for b in writes:
            b.last_w = ev
            b.readers = []
        for b in reads:
            if b not in writes:
                b.readers.append(ev)

    def op(self, eng, fn, reads=(), writes=()):
        waits = self._deps(eng, reads, writes)
        key = "eng_" + eng
        self.count[key] += 1
        val = self.count[key]
        sems = self.sems

        def emit(e, waits=waits, fn=fn, key=key):
            for k, v in waits:
                e.wait_ge(sems[k], v)
            inst = fn(e)
            inst.then_inc(sems[key], 1)
        self.lists[eng].append(emit)
        ev = (key, val)
        self._record(ev, reads, writes)
        return ev

    def dma(self, queue, semkey, fn, reads=(), writes=(), n=1):
        waits = self._deps(queue, reads, writes)
        self.count[semkey] += 16 * n
        val = self.count[semkey]
        sems = self.sems

        def emit(e, waits=waits, fn=fn, semkey=semkey):
            for k, v in waits:
                e.wait_ge(sems[k], v)
            fn(e, sems[semkey])
        self.lists[queue].append(emit)
        ev = (semkey, val)
        self._record(ev, reads, writes)
        return ev

    def final_wait(self, eng, events):
        sems = self.sems
        best = {}
        for k, v in events:
            best[k] = max(best.get(k, 0), v)

        def emit(e):
            for k, v in best.items():
                e.wait_ge(sems[k], v)
        self.lists[eng].append(emit)

    def emit_all(self):
        lists = self.lists
        with self.nc.Block() as block:
            @block.tensor
            def _(e):
                for f in lists["pe"]:
                    f(e)

            @block.scalar
            def _(e):
                for f in lists["act"]:
                    f(e)

            @block.vector
            def _(e):
                for f in lists["dve"]:
                    f(e)

            @block.gpsimd
            def _(e):
                for f in lists["pool"]:
                    f(e)

            @block.sync
            def _(e):
                for f in lists["sp"]:
                    f(e)


def interleave(*gens):
    gens = [g for g in gens if g is not None]
    alive = list(gens)
    while alive:
        nxt = []
        for g in alive:
            try:
                next(g)
                nxt.append(g)
            except StopIteration:
                pass
        alive = nxt


def interleave_w(pairs):
    st = [[g, float(r), 0.0] for g, r in pairs if g is not None]
    while st:
        nxt = []
        for ent in st:
            g, r, acc = ent
            acc += r
            dead = False
            while acc >= 1.0 - 1e-9:
                acc -= 1.0
                try:
                    next(g)
                except StopIteration:
                    dead = True
                    break
            ent[2] = acc
            if not dead:
                nxt.append(ent)
        st = nxt


def chain(*gens):
    for g in gens:
        if g is None:
            continue
        for _ in g:
            yield


def build_nc():
    nc = bass.Bass("TRN2", target_bir_lowering=False)

    def din(name, shape, dt=F32):
        return nc.dram_tensor(name, list(shape), dt, kind="ExternalInput").ap()

    def dout(name, shape, dt=F32):
        return nc.dram_tensor(name, list(shape), dt, kind="ExternalOutput").ap()

    xp = din("xp", [SEQ, D])
    xs = din("xs", [128, D])
    s0 = din("s0", [16, 4, 128, 128])
    norm_g = din("norm_g", [D])
    w_in = din("w_in", [D, DIN])
    lb_logits = din("lb_logits", [2, 512])
    hgrn_g = din("hgrn_g", [512])
    ln_g = din("ln_g", [512])
    ln_b = din("ln_b", [512])
    w_s = din("w_s", [4, 128, 128])
    b_s = din("b_s", [4, 128])
    w_out = din("w_out", [D, D])
    fin_g = din("fin_g", [D])
    c_idb = din("c_idb", [128, 128], BF16)
    c_idf = din("c_idf", [128, 128])
    c_maskP = din("c_maskP", [128, 512], U8)
    c_maskS = din("c_maskS", [128, 512], U8)
    c_maskPf = din("c_maskPf", [128, 128])
    c_BDf = din("c_BDf", [128, 128])
    c_resetP = din("c_resetP", [128, 1024], BF16)
    c_resetS = din("c_resetS", [128, 128], BF16)
    c_R8 = din("c_R8", [8, 128])
    c_mcol = din("c_mcol", [128, 16])

    yp = dout("yp", [SEQ, D])
    ys = dout("ys", [128, D])
    sp_o = dout("sp_o", [4, 128, 128])
    ss_o = dout("ss_o", [16, 4, 128, 128])
    vp_o = dout("vp_o", [128, 512])
    vs_o = dout("vs_o", [128, 512])

    with ExitStack() as es:
        S = Sched(nc, es)
        total = [0]
        dsz = {F32: 4, BF16: 2, U8: 1}

        def sb(name, shape, dt=F32):
            n = 1
            for s_ in shape[1:]:
                n *= s_
            total[0] += n * dsz[dt]
            return es.enter_context(nc.sbuf_tensor(name, list(shape), dt))

        banks = [es.enter_context(nc.psum_tensor("bank%d" % i, [128, 512], F32)) for i in range(8)]
        BK = [Buf("bank%d" % i) for i in range(8)]
        BKH = [[Buf("bank%d_h%d" % (i, h)) for h in range(2)] for i in range(8)]

        w_in_sb = sb("w_in_sb", [128, 8, DIN], BF16)
        w_out_sb = sb("w_out_sb", [128, 8, D], BF16)
        Bw_in = [[Buf("w_in_%d_%d" % (g, c)) for c in range(8)] for g in range(7)]
        Bw_out = [[Buf("w_out_%d_%d" % (hf, c)) for c in range(8)] for hf in range(2)]

        xin = [sb("xin%d" % i, [128, D]) for i in range(2)]
        Bxin = [Buf("xin%d" % i) for i in range(2)]
        xres = [sb("xres%d" % i, [128, D]) for i in range(2)]
        Bxres = [Buf("xres%d" % i) for i in range(2)]
        xn = [sb("xn%d" % i, [128, D], BF16) for i in range(1)]
        Bxn = [Buf("xn%d" % i) for i in range(1)]
        hT = [sb("hT%d" % i, [128, 8, 256], BF16) for i in range(2)]
        BhT = [[Buf("hT%d_%d" % (i, j)) for j in range(2)] for i in range(2)]

        Tt = sb("Tt", [128, 4, 256])
        kk = sb("kk", [128, 4, 256], BF16)
        bb = sb("bb", [128, 4, 256])
        E1 = sb("E1", [128, 4, 256])
        qs = sb("qs", [128, 4, 256], BF16)
        BT = [Buf("T%d" % h) for h in range(4)]
        Bkk = [Buf("kk%d" % h) for h in range(4)]
        Bbb = [Buf("bb%d" % i) for i in range(8)]
        BE1 = [Buf("E1_%d" % i) for i in range(8)]
        Bqs = [Buf("qs%d" % i) for i in range(2)]
        rcol = sb("rcol", [128, 4, 2])
        nrcol = sb("nrcol", [128, 4, 2])
        Brcol = Buf("rcol")
        zero_col = sb("zero_col", [128, 1])
        dcolS = sb("dcolS", [128, 4, 16])
        BdcolS = Buf("dcolS")

        def dbl(name, shape, dt):
            return [sb("%s%d" % (name, i), shape, dt) for i in range(2)]
        Qt = dbl("Qt", [128, 4, 256], BF16)
        Kt = dbl("Kt", [128, 4, 256], BF16)
        Kh = dbl("Kh", [128, 4, 256], BF16)
        sza = dbl("sza", [128, 4, 256], BF16)
        gu = dbl("gu", [128, 4, 256], BF16)
        szb = dbl("szb", [128, 4, 256], BF16)
        ercol = dbl("ercol", [128, 4, 2], F32)
        dcol = dbl("dcol", [128, 4, 2], F32)
        BQt = [Buf("Qt%d" % i) for i in range(2)]
        BKt = [Buf("Kt%d" % i) for i in range(2)]
        BKhat = [Buf("Kh%d" % i) for i in range(2)]
        Bsza = [[Buf("sza%d_%d" % (i, j)) for j in range(2)] for i in range(2)]
        Bgu = [[Buf("gu%d_%d" % (i, j)) for j in range(2)] for i in range(2)]
        Bszb = [[Buf("szb%d_%d" % (i, j)) for j in range(2)] for i in range(2)]
        Bercol = [Buf("ercol%d" % i) for i in range(2)]
        Bdcol = [Buf("dcol%d" % i) for i in range(2)]
        BE1keep = [Buf("E1keep%d" % i) for i in range(2)]

        Vt = [sb("Vt%d" % i, [128, 512], BF16) for i in range(4)]
        BVt = [Buf("Vt%d" % i) for i in range(4)]
        vhat = [sb("vhat%d" % i, [128, 512], BF16) for i in range(4)]
        Bvhat = [Buf("vhat%d" % i) for i in range(4)]
        gv = [sb("gv%d" % i, [128, 512]) for i in range(2)]
        Bgv = [Buf("gv%d" % i) for i in range(2)]
        stat = [sb("stat%d" % i, [128, 6]) for i in range(2)]
        mv = [sb("mv%d" % i, [128, 2]) for i in range(2)]
        vrs = [sb("vrs%d" % i, [128, 2]) for i in range(2)]
        Bstat = [Buf("stat%d" % i) for i in range(2)]
        Bmv = [Buf("mv%d" % i) for i in range(2)]
        Bvrs = [Buf("vrs%d" % i) for i in range(2)]
        mhalf = sb("mhalf", [128, 4])

        sTm = [sb("sTm%d" % i, [128, 512], BF16) for i in range(3)]
        BsTm = [Buf("sTm%d" % i) for i in range(3)]
        KhT = [sb("KhT%d" % i, [128, 512], BF16) for i in range(2)]
        BKhT = [Buf("KhT%d" % i) for i in range(2)]
        Sst = sb("Sst", [128, 512])
        BS = Buf("S")
        Spb = [sb("Spb%d" % i, [128, 512], BF16) for i in range(2)]
        BSpb = [Buf("Spb%d" % i) for i in range(2)]
        on = [sb("on%d" % i, [128, 512], BF16) for i in range(2)]
        Bon = [Buf("on%d" % i) for i in range(2)]
        sso = [sb("sso%d" % i, [128, 4]) for i in range(2)]
        rso = [sb("rso%d" % i, [128, 4]) for i in range(2)]
        Bsso = [Buf("sso%d" % i) for i in range(2)]
        Brso = [Buf("rso%d" % i) for i in range(2)]
        mixT = [sb("mixT%d" % i, [128, 8, 128], BF16) for i in range(2)]
        BmixA = [Buf("mixA%d" % i) for i in range(2)]
        BmixB = [Buf("mixB%d" % i) for i in range(2)]
        t1 = [sb("t1_%d" % i, [128, 512], BF16) for i in range(2)]
        Bt1 = [Buf("t1_%d" % i) for i in range(2)]
        Pm = [sb("Pm%d" % i, [128, 512], BF16) for i in range(2)]
        BPm = [Buf("Pm%d" % i) for i in range(2)]
        res = [sb("res%d" % i, [128, D]) for i in range(2)]
        Bres = [Buf("res%d" % i) for i in range(2)]
        ssx = [sb("ssx%d" % i, [128, 1]) for i in range(2)]
        rsx = [sb("rsx%d" % i, [128, 1]) for i in range(2)]
        Bssx = [Buf("ssx%d" % i) for i in range(2)]
        Brsx = [Buf("rsx%d" % i) for i in range(2)]
        ssf = [sb("ssf%d" % i, [128, 1]) for i in range(2)]
        rsf = [sb("rsf%d" % i, [128, 1]) for i in range(2)]
        Bssf = [Buf("ssf%d" % i) for i in range(2)]
        Brsf = [Buf("rsf%d" % i) for i in range(2)]

        idb = sb("idb", [128, 128], BF16)
        idf = sb("idf", [128, 128])
        maskP = sb("maskP", [128, 512], U8)
        maskS = sb("maskS", [128, 512], U8)
        BDf = sb("BDf", [128, 128])
        resetP = sb("resetP", [128, 1024], BF16)
        resetS = sb("resetS", [128, 128], BF16)
        R8 = sb("R8", [8, 128])
        mcol = sb("mcol", [128, 16])
        ngc = sb("ngc", [128, 8])
        lbl = sb("lbl", [128, 2, 4])
        lncol = sb("lncol", [128, 4, 4])
        ghn = sb("ghn", [128, 4])
        gln = sb("gln", [128, 4])
        gf_bc = sb("gf_bc", [128, D])
        g_bc = sb("g_bc", [128, 512])
        b_bc = sb("b_bc", [128, 512])
        ones1 = sb("ones1", [1, 128])
        WcTb = sb("WcTb", [128, 512], BF16)
        WbdTb = sb("WbdTb", [128, 512], BF16)
        C_P = sb("C_P", [128, 512])
        C_S = sb("C_S", [128, 512])
        Bconst = Buf("const")
        Bcols = Buf("cols")
        BW = Buf("Wprep")

        bsrow = Sst[0:1, :]
        bsrowS = res[1][0:1, 512:1024]
        Z8 = C_S[0:8, :].rearrange("p (h t) -> p h t", h=4)
        maskPf_sc = KhT[0][:].bitcast(F32)[:, 0:128]
        s0buf = [sb("s0buf%d" % i, [128, 512]) for i in range(3)]
        Bs0buf = [Buf("s0buf%d" % i) for i in range(3)]
        s0b = [sb("s0b%d" % i, [128, 512], BF16) for i in range(2)]
        Bs0b = [Buf("s0b%d" % i) for i in range(2)]
        KhTm = [sb("KhTm%d" % i, [128, 512], BF16) for i in range(2)]
        BKhTm = [Buf("KhTm%d" % i) for i in range(2)]
        oT_sb = s0buf[1]
        BoT = Bs0buf[1]

        assert total[0] < 212800, total[0]

        sem_c = S.dmasem("const")
        sem_c2 = S.dmasem("const2")
        sem_x = [S.dmasem("xin%d" % i) for i in range(2)]
        sem_xr = [S.dmasem("xres%d" % i) for i in range(2)]
        sem_y = [S.dmasem("y%d" % i) for i in range(2)]
        sem_s0 = [S.dmasem("s0_%d" % i) for i in range(3)]
        sem_so = [S.dmasem("so_%d" % i) for i in range(3)]
        sem_misc = S.dmasem("misc_out")
        sem_vo = [S.dmasem("vo%d" % i) for i in range(2)]
        sem_spo = S.dmasem("spo")
        out_events = []
        pend_st = []

        def flush_stores(keep=0):
            while len(pend_st) > keep:
                pend_st.pop(0)()
        STQ = "sp"

        def f32view(t, pat):
            return t[:].rearrange(pat).bitcast(F32)
        wst = [xres[0][:, 0:512], xres[0][:, 512:1024], xres[1][:, 0:512], xres[1][:, 512:1024],
               s0buf[0][:], s0buf[1][:], s0buf[2][:],
               f32view(Qt[1], "p h t -> p (h t)"), f32view(Kt[1], "p h t -> p (h t)"), f32view(Kh[1], "p h t -> p (h t)"),
               f32view(sza[1], "p h t -> p (h t)"), f32view(gu[1], "p h t -> p (h t)"), f32view(szb[1], "p h t -> p (h t)"),
               f32view(mixT[0], "p c t -> p (c t)"), f32view(mixT[1], "p c t -> p (c t)")]
        wst_alias = [[Bxres[0]], [Bxres[0]], [Bxres[1]], [Bxres[1]],
                     [Bs0buf[0]], [Bs0buf[1]], [Bs0buf[2]],
                     [BQt[1]], [BKt[1]], [BKhat[1]], Bsza[1], Bgu[1], Bszb[1],
                     [BmixA[0], BmixB[0]], [BmixA[1], BmixB[1]]]
        NSLOT = len(wst)
        Bwst = [Buf("wst%d" % i) for i in range(NSLOT)]
        sem_w = [S.dmasem("wstg%d" % i) for i in range(NSLOT)]
        wcnt = [0]

        def load_w(src_ap, dst_ap, Bdst, scale_ap):
            k = wcnt[0] % NSLOT
            wcnt[0] += 1
            al = wst_alias[k]
            S.dma("sp", sem_w[k], lambda e, sem, k=k, src_ap=src_ap: e.dma_start(out=wst[k], in_=src_ap).then_inc(sem, 16),
                  writes=[Bwst[k]] + al)
            if wcnt[0] % 2 == 0:
                if scale_ap is None:
                    S.op("dve", lambda e, k=k: e.tensor_copy(dst_ap, wst[k]), reads=[Bwst[k]] + al, writes=[Bdst])
                else:
                    S.op("dve", lambda e, k=k: e.tensor_scalar(dst_ap, wst[k], scale_ap, None, ALU.mult),
                         reads=[Bwst[k], Bngc] + al, writes=[Bdst])
            else:
                if scale_ap is None:
                    S.op("act", lambda e, k=k: e.activation(dst_ap, wst[k], AF.Copy), reads=[Bwst[k]] + al, writes=[Bdst])
                else:
                    S.op("act", lambda e, k=k: e.activation(dst_ap, wst[k], AF.Copy, scale=scale_ap),
                         reads=[Bwst[k], Bngc] + al, writes=[Bdst])

        def gen_load_w_in(groups):
            for g in groups:
                for c in range(8):
                    load_w(w_in[128 * c:128 * c + 128, 512 * g:512 * g + 512], w_in_sb[:, c, 512 * g:512 * g + 512],
                           Bw_in[g][c], ngc[:, c:c + 1])
                    yield

        def gen_load_w_out():
            for hf in range(2):
                for c in range(8):
                    load_w(w_out[128 * c:128 * c + 128, 512 * hf:512 * hf + 512], w_out_sb[:, c, 512 * hf:512 * hf + 512],
                           Bw_out[hf][c], None)
                    yield

        def midx(m):
            return {0: 0, NMT_P: 1}.get(m, m + 1)

        def tidx(n):
            return {0: 0, 1: 1, NTILE_P: 2}.get(n, n + 1)

        def tiles_of(m):
            return [2 * m, 2 * m + 1] if m < NMT_P else [NTILE_P]

        W_ORDER = [G_F, G_Q, G_ZA, G_ZB, G_U, G_V, G_I]
        w_loaded = set()

        def ensure_w(key):
            if key in w_loaded:
                return
            w_loaded.add(key)
            if key == "out":
                for _ in gen_load_w_out():
                    pass
            else:
                for _ in gen_load_w_in([key]):
                    pass

        def tile_rows(n):
            return (xp[128 * n:128 * n + 128, :], yp[128 * n:128 * n + 128, :]) if n < NTILE_P else (xs, ys)

        def pow_rsqrt(dst, src, Bdst, Bsrc, scale, width):
            S.op("dve", lambda e: e.tensor_scalar(dst, src, scale, EPS, ALU.mult, ALU.add), reads=[Bsrc], writes=[Bdst])
            S.op("pool", lambda e: e.tensor_tensor(dst, dst, mhalf[:, 0:width], ALU.pow), reads=[Bdst, Bmh], writes=[Bdst])

        def gen_xn(n, queue="sp"):
            m, j = (n // 2, n % 2) if n < NTILE_P else (NMT_P, 0)
            mb = midx(m) % 2
            xb = 0
            xi = tidx(n) % 2
            src, _ = tile_rows(n)
            S.dma(queue, sem_x[xi], lambda e, sem: e.dma_start(out=xin[xi][:], in_=src).then_inc(sem, 16), writes=[Bxin[xi]])
            yield
            S.op("act", lambda e: e.activation(xn[xb][:], xin[xi][:], AF.Square, accum_out=ssx[xi][:]), reads=[Bxin[xi]], writes=[Bssx[xi], Bxn[xb]])
            S.op("pool", lambda e: e.tensor_scalar(rsx[xi][:], ssx[xi][:], 1.0 / D, EPS, ALU.mult, ALU.add), reads=[Bssx[xi]], writes=[Brsx[xi]])
            S.op("pool", lambda e: e.tensor_tensor(rsx[xi][:], rsx[xi][:], mhalf[:, 0:1], ALU.pow), reads=[Brsx[xi], Bmh], writes=[Brsx[xi]])
            if tidx(n) < 2:
                S.op("dve", lambda e: e.tensor_scalar(xn[xb][:], xin[xi][:], rsx[xi][:], None, ALU.mult), reads=[Bxin[xi], Brsx[xi]], writes=[Bxn[xb]])
            else:
                S.op("pool", lambda e: e.tensor_scalar(xn[xb][:], xin[xi][:], rsx[xi][:], 1.0, ALU.mult, ALU.mult), reads=[Bxin[xi], Brsx[xi]], writes=[Bxn[xb]])
            yield
            yield
            yield
            xbk = next_bank()
            pbank = banks[xbk][:].bitcast(BF16)

            def tr(e):
                last = None
                for c in range(8):
                    last = e.transpose(pbank[:, 128 * c:128 * c + 128], xn[xb][:, 128 * c:128 * c + 128], idb[:])
                return last
            S.op("pe", tr, reads=[Bxn[xb], Bidb], writes=[BK[xbk]])
            dst = hT[mb][:, :, 128 * j:128 * j + 128]
            S.op("dve", lambda e: e.tensor_copy(dst, pbank.rearrange("p (c t) -> p c t", c=8)), reads=[BK[xbk]], writes=[BhT[mb][j]])
            yield

        proj_rot = [0]

        def next_bank():
            k = proj_rot[0] % 3
            proj_rot[0] += 1
            return k

        def fm_view(t, NT):
            return t[:].rearrange("p h t -> p (h t)")[:, 0:4 * NT].rearrange("p (h t) -> p h t", h=4)

        def gen_proj(m, first=False):
            NT = 256 if m < NMT_P else 128
            ntl = NT // 128
            mb = midx(m) % 2
            hTm = hT[mb]
            BhTm = [BhT[mb][j] for j in range(ntl)]
            tiles = tiles_of(m)
            wpos = [0]

            def prefetch_w(k):
                if first:
                    for g in W_ORDER[wpos[0]:wpos[0] + k]:
                        ensure_w(g)
                    wpos[0] += k
            prefetch_w(3)

            def fm_group(g, evac):
                for bp in range(2):
                    bk = next_bank()

                    def mm(e, bp=bp, bk=bk):
                        last = None
                        for bl in range(2):
                            col0 = 512 * g + 128 * (2 * bp + bl)
                            for c in range(8):
                                last = e.matmul(banks[bk][:, bl * NT:(bl + 1) * NT], w_in_sb[:, c, col0:col0 + 128], hTm[:, c, 0:NT],
                                                start=(c == 0), stop=(c == 7))
                        return last
                    S.op("pe", mm, reads=BhTm + Bw_in[g], writes=[BK[bk]])
                    evac(bp, banks[bk][:, 0:2 * NT], BK[bk])
                    if bp == 0:
                        prefetch_w(1)
                    yield

            def dst2(t, bp):
                return fm_view(t, NT)[:, 2 * bp:2 * bp + 2, :].rearrange("p a t -> p (a t)")

            Tv, kkv, bbv, E1v, qsv = (fm_view(t, NT) for t in (Tt, kk, bb, E1, qs))
            prompt = m < NMT_P
            fT = (lambda h: [BT[h]]) if prompt else (lambda h: BT)
            fkk = (lambda h: [Bkk[h]]) if prompt else (lambda h: Bkk)
            fqs = (lambda bp: [Bqs[bp]]) if prompt else (lambda bp: Bqs)
            fza = (lambda bp: [Bsza[mb][bp]]) if prompt else (lambda bp: Bsza[mb])
            fzb = (lambda bp: [Bszb[mb][bp]]) if prompt else (lambda bp: Bszb[mb])
            fgu = (lambda bp: [Bgu[mb][bp]]) if prompt else (lambda bp: Bgu[mb])
            G = 128 if prompt else 8
            NG = NT // G

            def ev_f(bp, src, Bsrc):
                S.op("act", lambda e: e.activation(dst2(Tt, bp), src, AF.Tanh, scale=0.5), reads=[Bsrc], writes=fT(2 * bp) + fT(2 * bp + 1))
            yield from fm_group(G_F, ev_f)

            for h in range(4):
                S.op("dve", lambda e, h=h: e.tensor_scalar(kkv[:, h, :], Tv[:, h, :], lncol[:, h, 2:3], lncol[:, h, 3:4], ALU.mult, ALU.add),
                     reads=fT(h) + [Bcols], writes=fkk(h))
            for h in range(4):
                S.op("act", lambda e, h=h: e.activation(Tv[:, h, :], Tv[:, h, :], AF.Ln, scale=lncol[:, h, 0:1], bias=lncol[:, h, 1:2]),
                     reads=fT(h) + fkk(h) + [Bcols], writes=fT(h))

            def ev_q(bp, src, Bsrc):
                S.op("act", lambda e: e.activation(dst2(qs, bp), src, AF.Silu), reads=[Bsrc], writes=fqs(bp))
            yield from fm_group(G_Q, ev_q)

            if prompt:
                S.op("dve", lambda e: e.tensor_tensor_scan(bb[:].rearrange("p h t -> p (h t)"), resetP[:], Tt[:].rearrange("p h t -> p (h t)"),
                                                          0.0, ALU.mult, ALU.add), reads=BT + [Bconst], writes=Bbb)
                bb4 = bb[:].rearrange("p h (j t) -> p h j t", j=2)
                S.op("dve", lambda e: e.tensor_copy(rcol[:], bb4[:, :, :, 63]), reads=Bbb, writes=[Brcol])
            else:
                for h in range(4):
                    S.op("dve", lambda e, h=h: e.tensor_tensor_scan(bbv[:, h, :], resetS[:], Tv[:, h, :], 0.0, ALU.mult, ALU.add),
                         reads=BT + [Bconst, Bc2, Bc3], writes=Bbb)

            def ev_za(bp, src, Bsrc):
                S.op("act", lambda e: e.activation(dst2(sza[mb], bp), src, AF.Silu), reads=[Bsrc], writes=fza(bp))
            yield from fm_group(G_ZA, ev_za)

            def ev_zb(bp, src, Bsrc):
                S.op("act", lambda e: e.activation(dst2(szb[mb], bp), src, AF.Silu), reads=[Bsrc], writes=fzb(bp))
            yield from fm_group(G_ZB, ev_zb)

            if prompt:
                S.op("act", lambda e: e.activation(ercol[mb][:], rcol[:], AF.Exp), reads=[Brcol], writes=[Bercol[mb]])
                bb4v = bb[:].rearrange("p h (j t) -> p h j t", j=2)
                E14v = E1[:].rearrange("p h (j t) -> p h j t", j=2)
                S.op("dve", lambda e: e.tensor_tensor(E14v, bb4v, rcol[:].unsqueeze(3).to_broadcast([128, 4, 2, 128]), ALU.subtract),
                     reads=Bbb + [Brcol], writes=BE1)
                bblast = bb4v[:, :, :, G - 1:G].rearrange("p h j o -> p h (j o)")
                S.op("act", lambda e: e.activation(dcol[mb][:], bblast, AF.Exp), reads=Bbb, writes=[Bdcol[mb]])
                S.op("act", lambda e: e.activation(bb[:].rearrange("p h t -> p (h t)"), E1[:].rearrange("p h t -> p (h t)"), AF.Exp, scale=-1.0),
                     reads=BE1 + [Bdcol[mb]], writes=Bbb)
                S.op("act", lambda e: e.activation(E1[:].rearrange("p h t -> p (h t)"), E1[:].rearrange("p h t -> p (h t)"), AF.Exp),
                     reads=BE1 + Bbb, writes=BE1)
            else:
                S.op("act", lambda e: e.activation(E1v.rearrange("p h t -> p (h t)"), bbv.rearrange("p h t -> p (h t)"), AF.Exp),
                     reads=Bbb, writes=BE1)
            f2 = lambda a: a.rearrange("p h t -> p (h t)")
            E1g = E1v.rearrange("p h (j t) -> p h j t", t=G)
            if not prompt:
                S.op("dve", lambda e: e.tensor_copy(dcolS[:], E1g[:, :, :, G - 1:G].rearrange("p h j o -> p h (j o)")), reads=BE1, writes=[BdcolS])
            Qv, Kv, Khv = fm_view(Qt[mb], NT), fm_view(Kt[mb], NT), fm_view(Kh[mb], NT)
            S.op("pool", lambda e: e.tensor_tensor(f2(Qv), f2(qsv), f2(E1v), ALU.mult), reads=Bqs + BE1, writes=[BQt[mb]])

            if not prompt:
                S.op("act", lambda e: e.activation(bbv.rearrange("p h t -> p (h t)"), bbv.rearrange("p h t -> p (h t)"), AF.Exp, scale=-1.0),
                     reads=Bbb, writes=Bbb)
            S.op("pool", lambda e: e.tensor_tensor(f2(Kv), f2(kkv), f2(bbv), ALU.mult), reads=Bkk + Bbb, writes=[BKt[mb]])
            S.op("pool", lambda e: e.tensor_tensor(Khv.rearrange("p h (j t) -> p h j t", t=G), Kv.rearrange("p h (j t) -> p h j t", t=G),
                                                  E1g[:, :, :, G - 1:G].to_broadcast([128, 4, NG, G]), ALU.mult),
                 reads=[BKt[mb]] + BE1, writes=[BKhat[mb]])

            def ev_u(bp, src, Bsrc):
                S.op("act", lambda e: e.activation(dst2(gu[mb], bp), src, AF.Gelu_apprx_tanh), reads=[Bsrc], writes=fgu(bp))
            yield from fm_group(G_U, ev_u)

            for j, n in enumerate(tiles):
                bk = next_bank()
                vb = tidx(n) % 4
                gb = tidx(n) % 2

                def mmv(e, j=j, bk=bk):
                    last = None
                    for c in range(8):
                        last = e.matmul(banks[bk][:], hTm[:, c, 128 * j:128 * j + 128], w_in_sb[:, c, 512 * G_V:512 * G_V + 512],
                                        start=(c == 0), stop=(c == 7))
                    return last
                S.op("pe", mmv, reads=[BhTm[j]] + Bw_in[G_V], writes=[BK[bk]])
                S.op("act", lambda e, bk=bk, gb=gb: e.activation(gv[gb][:], banks[bk][:], AF.Gelu_apprx_tanh), reads=[BK[bk]], writes=[Bgv[gb]])
                yield
                S.op("dve", lambda e, gb=gb: e.bn_stats(stat[gb][:], gv[gb][:]), reads=[Bgv[gb]], writes=[Bstat[gb]])
                S.op("dve", lambda e, gb=gb: e.bn_aggr(mv[gb][:], stat[gb][:]), reads=[Bstat[gb]], writes=[Bmv[gb]])
                S.op("dve", lambda e, gb=gb: e.tensor_scalar(vrs[gb][:, 0:1], mv[gb][:, 1:2], EPS, None, ALU.add), reads=[Bmv[gb]], writes=[Bvrs[gb]])
                S.op("pool", lambda e, gb=gb: e.tensor_tensor(vrs[gb][:, 0:1], vrs[gb][:, 0:1], mhalf[:, 0:1], ALU.pow),
                     reads=[Bvrs[gb], Bmh], writes=[Bvrs[gb]])
                yield
                S.op("dve", lambda e, gb=gb: e.scalar_tensor_tensor(vrs[gb][:, 1:2], mv[gb][:, 0:1], -1.0, vrs[gb][:, 0:1], ALU.mult, ALU.mult),
                     reads=[Bmv[gb], Bvrs[gb]], writes=[Bvrs[gb]])
                yield
                S.op("act", lambda e, gb=gb, vb=vb: e.activation(vhat[vb][:], gv[gb][:], AF.Identity, scale=vrs[gb][:, 0:1], bias=vrs[gb][:, 1:2]),
                     reads=[Bgv[gb], Bvrs[gb]], writes=[Bvhat[vb]])
                if n == NTILE_P - 1 or n == NTILE_P:
                    dsto = vp_o if n == NTILE_P - 1 else vs_o
                    S.op("act", lambda e, gb=gb: e.activation(gv[gb][:], gv[gb][:], AF.Identity, scale=vrs[gb][:, 0:1], bias=vrs[gb][:, 1:2]),
                         reads=[Bgv[gb], Bvrs[gb], Bvhat[vb]], writes=[Bgv[gb]])
                    S.op("dve", lambda e, gb=gb: e.tensor_tensor(gv[gb][:], gv[gb][:], g_bc[:], ALU.mult), reads=[Bgv[gb], Bconst, Bc2, Bc3], writes=[Bgv[gb]])
                    S.op("dve", lambda e, gb=gb: e.tensor_tensor(gv[gb][:], gv[gb][:], b_bc[:], ALU.add), reads=[Bgv[gb], Bconst, Bc2, Bc3], writes=[Bgv[gb]])
                    out_events.append(S.dma(STQ, sem_vo[n - (NTILE_P - 1)], lambda e, sem, dsto=dsto, gb=gb: e.dma_start(out=dsto, in_=gv[gb][:]).then_inc(sem, 16),
                                            reads=[Bgv[gb]]))
                if j == 0:
                    prefetch_w(1)
                yield

            for j, n in enumerate(tiles):
                bk = next_bank()
                vb = tidx(n) % 4

                def mmi(e, j=j, bk=bk):
                    last = None
                    for c in range(8):
                        last = e.matmul(banks[bk][:], hTm[:, c, 128 * j:128 * j + 128], w_in_sb[:, c, 512 * G_I:512 * G_I + 512],
                                        start=(c == 0), stop=(c == 7))
                    return last
                S.op("pe", mmi, reads=[BhTm[j]] + Bw_in[G_I], writes=[BK[bk]])
                S.op("act", lambda e, bk=bk, vb=vb: e.activation(Vt[vb][:], banks[bk][:], AF.Copy), reads=[BK[bk]], writes=[BVt[vb]])
                yield
            if first:
                for g in W_ORDER:
                    ensure_w(g)

        hcount = [0]
        htb = {}

        def gen_Hf(n, flag=None):
            sample = (n == NTILE_P)
            m, j = (n // 2, n % 2) if not sample else (NMT_P, 0)
            mb = midx(m) % 2
            NT = 128 if sample else 256
            tb = hcount[0] % 2
            first_h = hcount[0] < 2
            hcount[0] += 1
            htb[n] = tb
            ob = 4 if tb == 0 else 6
            vb = tidx(n) % 4
            t0 = 128 * j
            Qv, Kv, Khv = fm_view(Qt[mb], NT), fm_view(Kt[mb], NT), fm_view(Kh[mb], NT)
            guv, szbv = fm_view(gu[mb], NT), fm_view(szb[mb], NT)
            sl = slice(t0, t0 + 128)
            src, dsty = tile_rows(n)
            hs = lambda h: slice(128 * h, 128 * h + 128)

            S.dma("sp", sem_xr[tb], lambda e, sem: e.dma_start(out=xres[tb][:], in_=src).then_inc(sem, 16),
                  writes=[Bxres[tb]])

            if (not sample) and j == 0:
                S.op("dve", lambda e: e.tensor_tensor(Spb[n % 2][:].rearrange("k (h v) -> k h v", h=4), Sst[:].rearrange("k (h v) -> k h v", h=4),
                                                      ercol[mb][:, :, 0:1].to_broadcast([128, 4, 128]), ALU.mult),
                     reads=[BS, Bercol[mb]], writes=[BSpb[n % 2]])
            def mm_s(e):
                last = None
                for h in range(4):
                    e.matmul(banks[3][0:64, 128 * h:128 * h + 128], Kv[:, h, t0:t0 + 64], Qv[:, h, sl], start=True, stop=True)
                    last = e.matmul(banks[3][64:128, 128 * h + 64:128 * h + 128], Kv[:, h, t0 + 64:t0 + 128], Qv[:, h, t0 + 64:t0 + 128],
                                    start=True, stop=True)
                return last
            S.op("pe", mm_s, reads=[BKt[mb], BQt[mb]], writes=[BK[3]])
            sti = 2 if sample else tb
            S.op("dve", lambda e: e.copy_predicated(sTm[sti][:], (maskS if sample else maskP)[:], banks[3][:]),
                 reads=[BK[3], Bconst, Bc2, Bc3], writes=[BsTm[sti]])
            pb = banks[5][:, 256:512].bitcast(BF16)

            def tr_k(e):
                last = None
                for h in range(4):
                    last = e.transpose(pb[:, hs(h)], Khv[:, h, sl], idb[:])
                return last
            S.op("pe", tr_k, reads=[BKhat[mb], Bidb], writes=[BK[5]])
            S.op("dve", lambda e: e.tensor_copy(KhT[tb][:], pb), reads=[BK[5]], writes=[BKhT[tb]])
            yield

            if not sample:
                sp_i = n % 2

                def emit_sprime(jj, idx):
                    S.op("dve", lambda e: e.tensor_tensor(Spb[idx][:].rearrange("k (h v) -> k h v", h=4), Sst[:].rearrange("k (h v) -> k h v", h=4),
                                                          ercol[mb][:, :, jj:jj + 1].to_broadcast([128, 4, 128]), ALU.mult),
                         reads=[BS, Bercol[mb]], writes=[BSpb[idx]])

                def mm_o(e):
                    last = None
                    for h in range(4):
                        e.matmul(banks[ob][:, hs(h)], Qv[:, h, sl], Spb[sp_i][:, hs(h)], start=True, stop=False)
                        last = e.matmul(banks[ob][:, hs(h)], sTm[tb][:, hs(h)], Vt[vb][:, hs(h)], start=False, stop=True)
                    return last
                S.op("pe", mm_o, reads=[BQt[mb], BSpb[sp_i], BsTm[tb], BVt[vb]], writes=[BK[ob]])

                def mm_u(e):
                    last = None
                    for h in range(4):
                        last = e.matmul(banks[5][:, hs(h)], KhT[tb][:, hs(h)], Vt[vb][:, hs(h)], start=True, stop=True)
                    return last
                S.op("pe", mm_u, reads=[BKhT[tb], BVt[vb]], writes=[BK[5]])
                for h in range(4):
                    S.op("dve", lambda e, h=h: e.scalar_tensor_tensor(Sst[:, hs(h)], Sst[:, hs(h)], dcol[mb][:, h, j:j + 1], banks[5][:, hs(h)],
                                                                      ALU.mult, ALU.add),
                         reads=[BS, Bdcol[mb], BK[5]], writes=[BS])
                if j == 0:
                    emit_sprime(1, 1 - sp_i)
                if n == NTILE_P - 1:
                    out_events.append(S.dma(STQ, sem_spo, lambda e, sem: e.dma_start(
                        out=sp_o.rearrange("h k v -> k h v"), in_=Sst[:].rearrange("k (h v) -> k h v", h=4)).then_inc(sem, 16), reads=[BS]))
                yield
            else:
                def mm_oi(e):
                    last = None
                    for h in range(4):
                        last = e.matmul(banks[ob][:, hs(h)], Vt[vb][:, hs(h)], sTm[2][:, hs(h)], start=(h == 0), stop=False, skip_group_check=True)
                    return last
                S.op("pe", mm_oi, reads=[BsTm[2], BVt[vb]], writes=[BK[ob]])
                so_pend = []
                def ld_s0(q):
                    sbi = q % 3
                    S.dma("sp", sem_s0[sbi], lambda e, sem: e.dma_start(
                        out=s0buf[sbi][:].rearrange("k (h v) -> k h v", h=4), in_=s0[q].rearrange("h k v -> k h v")).then_inc(sem, 16),
                        writes=[Bs0buf[sbi]])
                def prep_q(q):
                    sbi, cbi = q % 3, q % 2
                    S.op("act", lambda e: e.activation(s0b[cbi][:], s0buf[sbi][:], AF.Copy), reads=[Bs0buf[sbi]], writes=[Bs0b[cbi]])
                    S.op("dve", lambda e: e.tensor_scalar(KhTm[cbi][:], KhT[tb][:], mcol[:, q:q + 1], None, ALU.mult),
                         reads=[BKhT[tb], Bconst, Bc2, Bc3], writes=[BKhTm[cbi]])

                def st_s0(q):
                    sbi = q % 3
                    out_events.append(S.dma(STQ, sem_so[sbi], lambda e, sem: e.dma_start(
                        out=ss_o[q].rearrange("h k v -> k h v"), in_=s0buf[sbi][:].rearrange("k (h v) -> k h v", h=4)).then_inc(sem, 16),
                        reads=[Bs0buf[sbi]]))
                ld_s0(0)
                ld_s0(1)
                prep_q(0)
                for q in range(16):
                    sbi, cbi, ubank = q % 3, q % 2, (5 if q % 2 == 0 else 3)
                    if q >= 1:
                        st_s0(q - 1)
                    if q + 2 < 16:
                        ld_s0(q + 2)

                    def mm_q(e, q=q, cbi=cbi):
                        last = None
                        for h in range(4):
                            last = e.matmul(banks[ob][:, 128 * h + 8 * q:128 * h + 8 * q + 8], s0b[cbi][:, hs(h)], Qv[:, h, 8 * q:8 * q + 8],
                                            start=False, stop=(q == 15 and h == 3), skip_group_check=True)
                        return last
                    S.op("pe", mm_q, reads=[Bs0b[cbi], BQt[mb]], writes=[BK[ob]])

                    def mm_uq(e, cbi=cbi, ubank=ubank):
                        last = None
                        for h in range(4):
                            last = e.matmul(banks[ubank][:, hs(h)], KhTm[cbi][:, hs(h)], Vt[vb][:, hs(h)], start=True, stop=True)
                        return last
                    S.op("pe", mm_uq, reads=[BKhTm[cbi], BVt[vb]], writes=[BK[ubank]])
                    if q + 1 < 16:
                        prep_q(q + 1)
                    for h in range(4):
                        S.op("dve", lambda e, h=h, q=q, sbi=sbi, ubank=ubank: e.scalar_tensor_tensor(
                            s0buf[sbi][:, hs(h)], s0buf[sbi][:, hs(h)], dcolS[:, h, q:q + 1], banks[ubank][:, hs(h)], ALU.mult, ALU.add),
                            reads=[Bs0buf[sbi], BK[ubank], BdcolS], writes=[Bs0buf[sbi]])
                    yield
                st_s0(15)
                S.op("act", lambda e: e.activation(oT_sb[:], banks[ob][:], AF.Copy), reads=[BK[ob]], writes=[BoT])

                def tr_o(e):
                    last = None
                    for h in range(4):
                        last = e.transpose(banks[ob][:, hs(h)], oT_sb[:, hs(h)], idf[:])
                    return last
                S.op("pe", tr_o, reads=[BoT, Bconst, Bc2, Bc3, Bidf], writes=[BK[ob]])
                yield

            for h in range(4):
                S.op("act", lambda e, h=h: e.activation(on[tb][:, hs(h)], banks[ob][:, hs(h)], AF.Square, accum_out=sso[tb][:, h:h + 1]),
                     reads=[BK[ob]], writes=[Bsso[tb], Bon[tb]])
            pow_rsqrt(rso[tb][:], sso[tb][:], Brso[tb], Bsso[tb], 1.0 / 128, 4)
            yield
            S.op("dve", lambda e: e.tensor_tensor(on[tb][:].rearrange("t (h v) -> t h v", h=4), banks[ob][:].rearrange("t (h v) -> t h v", h=4),
                                                  rso[tb][:].unsqueeze(2).to_broadcast([128, 4, 128]), ALU.mult),
                 reads=[BK[ob], Brso[tb]], writes=[Bon[tb]])
            if flag is not None:
                flag["done"] = True
            yield

            Wb = WbdTb if sample else WcTb
            Cc = C_S if sample else C_P

            def mm_m(e):
                last = None
                for h in range(4):
                    last = e.matmul(banks[3][:, hs(h)], vhat[vb][:, hs(h)], Wb[:, hs(h)], start=True, stop=True)
                return last
            S.op("pe", mm_m, reads=[Bvhat[vb], BW], writes=[BK[3]])
            for h in range(4):
                S.op("dve", lambda e, h=h: e.scalar_tensor_tensor(t1[tb][:, hs(h)], banks[3][:, hs(h)], gln[:, h:h + 1], Cc[:, hs(h)], ALU.mult, ALU.add),
                     reads=[BK[3], Bcols, Bc2, Bc3, Bc4, BW], writes=[Bt1[tb]])
            S.op("pool", lambda e: e.tensor_tensor(Pm[tb][:].rearrange("p (h t) -> p h t", h=4), guv[:, :, sl], szbv[:, :, sl], ALU.mult),
                 reads=Bgu[mb] + Bszb[mb], writes=[BPm[tb]])
            S.op("pool", lambda e: e.tensor_tensor(mixT[tb][:, 4:8, :].rearrange("p h t -> p (h t)"), t1[tb][:], Pm[tb][:], ALU.mult),
                 reads=[Bt1[tb], BPm[tb]], writes=[BmixB[tb]])
            yield

        def gen_Hb(n):
            sample = (n == NTILE_P)
            m, j = (n // 2, n % 2) if not sample else (NMT_P, 0)
            mb = midx(m) % 2
            NT = 128 if sample else 256
            tb = htb[n]
            szav = fm_view(sza[mb], NT)
            sl = slice(128 * j, 128 * j + 128)
            src, dsty = tile_rows(n)
            hs = lambda h: slice(128 * h, 128 * h + 128)
            pb5 = banks[7][:, 0:256].bitcast(BF16)

            def tr_on(e):
                last = None
                for h in range(4):
                    last = e.transpose(pb5[:, hs(h)], on[tb][:, hs(h)], idb[:])
                return last
            S.op("pe", tr_on, reads=[Bon[tb], Bidb], writes=[BK[7]])
            for h in range(4):
                S.op("dve", lambda e, h=h: e.scalar_tensor_tensor(mixT[tb][:, h, :], pb5[:, hs(h)], ghn[:, h:h + 1], szav[:, h, sl], ALU.mult, ALU.mult),
                     reads=[BK[7], Bcols, Bc2, Bc3, Bc4] + (Bsza[mb] if sample else [Bsza[mb][h // 2]]), writes=[BmixA[tb]])
            yield

            flush_stores()
            for hf in range(2):
                bko = 7 if hf == 0 else 3

                def mm_out(e, hf=hf, bko=bko):
                    last = None
                    for c in range(8):
                        last = e.matmul(banks[bko][:], mixT[tb][:, c, :], w_out_sb[:, c, 512 * hf:512 * hf + 512], start=(c == 0), stop=(c == 7))
                    return last
                S.op("pe", mm_out, reads=[BmixA[tb], BmixB[tb]] + Bw_out[hf], writes=[BK[bko]])
                S.op("dve", lambda e, hf=hf, bko=bko: e.tensor_tensor(res[tb][:, 512 * hf:512 * hf + 512], banks[bko][:], xres[tb][:, 512 * hf:512 * hf + 512], ALU.add),
                     reads=[BK[bko], Bxres[tb]], writes=[Bres[tb]])
                yield
            S.op("act", lambda e: e.activation(mixT[tb][:].rearrange("p c t -> p (c t)"), res[tb][:], AF.Square, accum_out=ssf[tb][:]), reads=[Bres[tb]], writes=[Bssf[tb], BmixA[tb], BmixB[tb]])
            pow_rsqrt(rsf[tb][:], ssf[tb][:], Brsf[tb], Bssf[tb], 1.0 / D, 1)
            yield
            S.op("act", lambda e: e.activation(res[tb][:], res[tb][:], AF.Copy, scale=rsf[tb][:]), reads=[Bres[tb], Brsf[tb]], writes=[Bres[tb]])
            S.op("pool", lambda e: e.tensor_tensor(res[tb][:], res[tb][:], gf_bc[:], ALU.mult), reads=[Bres[tb], Bconst, Bc2, Bc3], writes=[Bres[tb]])
            pend_st.append(lambda: out_events.append(S.dma(STQ, sem_y[tb], lambda e, sem: e.dma_start(out=dsty, in_=res[tb][:]).then_inc(sem, 16), reads=[Bres[tb]])))
            yield

        def run(g):
            for _ in g:
                pass

        def igen(*gens):
            alive = [g for g in gens if g is not None]
            while alive:
                nxt = []
                for g in alive:
                    try:
                        next(g)
                        nxt.append(g)
                        yield
                    except StopIteration:
                        pass
                alive = nxt

        RATE_H, RATE_X = _DBG.get("rate_h", 1.25), _DBG.get("rate_x", 1.0)
        MT_ORDER = [0, NMT_P] + list(range(1, NMT_P))
        Bmh = Buf("mhalf")
        Bngc = Buf("ngc")
        sem_c0 = S.dmasem("const0")
        S.op("pool", lambda e: e.memset(mhalf[:], -0.5), writes=[Bmh])
        S.op("pool", lambda e: e.memset(ones1[:], 1.0), writes=[Bmh])
        S.op("pool", lambda e: e.memset(zero_col[:], 0.0), writes=[Bmh])
        for i in range(3):
            S.op("pool", lambda e, i=i: e.memset(sTm[i][:], 0.0), writes=[BsTm[i]])
        g_x0 = [gen_xn(n, queue="act") for n in tiles_of(MT_ORDER[0])]
        Bidf = Buf("idf")
        Bc4 = Buf("cols_t")
        Bstg = Buf("colstg")
        sem_c5 = S.dmasem("const5")
        colstg = KhT[1][:].bitcast(F32)[0:24, 0:128]

        def ld_rows(e, sem):
            stg = KhT[1][:].bitcast(F32)
            e.dma_start(out=idf[:], in_=c_idf).then_inc(sem, 16)
            e.dma_start(out=stg[0:8, 0:128], in_=norm_g.rearrange("(c p) -> c p", p=128)).then_inc(sem, 16)
            e.dma_start(out=stg[8:16, 0:128], in_=lb_logits.rearrange("r (h k) -> (r h) k", k=128)).then_inc(sem, 16)
            e.dma_start(out=stg[16:20, 0:128], in_=hgrn_g.rearrange("(h v) -> h v", v=128)).then_inc(sem, 16)
            e.dma_start(out=stg[20:24, 0:128], in_=ln_g.rearrange("(h e) -> h e", e=128)).then_inc(sem, 16)
        S.dma("sp", sem_c5, ld_rows, writes=[Bstg, BKhT[1], Bidf], n=5)
        S.op("pe", lambda e: e.transpose(banks[6][:, 0:24], colstg, idf[0:24, 0:24]), reads=[Bstg, Bidf], writes=[BK[6]])
        S.op("dve", lambda e: e.tensor_copy(ngc[:], banks[6][:, 0:8]), reads=[BK[6]], writes=[Bngc])
        S.op("dve", lambda e: e.tensor_copy(lbl[:].rearrange("p r h -> p (r h)"), banks[6][:, 8:16]), reads=[BK[6]], writes=[Bcols])
        S.op("dve", lambda e: e.tensor_copy(ghn[:], banks[6][:, 16:20]), reads=[BK[6]], writes=[Bc4])
        S.op("dve", lambda e: e.tensor_copy(gln[:], banks[6][:, 20:24]), reads=[BK[6]], writes=[Bc4])
        next(g_x0[0])
        next(g_x0[1])
        Bidb = Buf("idb")
        sem_c4 = S.dmasem("const4")
        S.dma("act", sem_c4, lambda e, sem: e.dma_start(out=idb[:], in_=c_idb).then_inc(sem, 16), writes=[Bidb])
        next(g_x0[0])
        Bc2, Bc3 = Buf("const_late"), Buf("cols_late")
        sem_c3 = S.dmasem("const3")

        def ld_urgent(e, sem):
            e.dma_start(out=resetP[:], in_=c_resetP).then_inc(sem, 16)
        S.dma("act", sem_c3, ld_urgent, writes=[Bconst], n=1)
        def ld_cols(e, sem):
            def d(o, i):
                e.dma_start(out=o, in_=i, allow_slow_non_contiguous=True).then_inc(sem, 16)
            for h in range(4):
                d(Z8[:, h, :].rearrange("t (j s) -> t j s", j=16), w_s[h, 0:8, 0:8].unsqueeze(1).broadcast_to([8, 16, 8]))
            d(bsrowS.rearrange("o (h j t) -> o h j t", h=4, j=16), b_s[:, 0:8].unsqueeze(1).broadcast_to([4, 16, 8]).unsqueeze(0))
        S.dma("act", sem_c2, ld_cols, writes=[Bc3, Bres[1], BW], n=5)

        run(g_x0[0])
        next(g_x0[1])
        ensure_w(W_ORDER[0])
        run(g_x0[1])
        ensure_w(W_ORDER[1])
        S.op("dve", lambda e: e.tensor_tensor(lncol[:, :, 0], lbl[:, 0, :], lbl[:, 1, :], ALU.subtract), reads=[Bcols], writes=[Bcols])
        S.op("act", lambda e: e.activation(lncol[:, :, 1], lncol[:, :, 0], AF.Tanh, scale=0.5), reads=[Bcols], writes=[Bcols])
        S.op("dve", lambda e: e.tensor_scalar(lncol[:, :, 3], lncol[:, :, 1], -0.25, 0.25, ALU.mult, ALU.add), reads=[Bcols], writes=[Bcols])
        S.op("dve", lambda e: e.tensor_scalar(lncol[:, :, 2], lncol[:, :, 1], 0.25, -0.25, ALU.mult, ALU.add), reads=[Bcols], writes=[Bcols])
        S.op("dve", lambda e: e.tensor_scalar(lncol[:, :, 0], lncol[:, :, 1], -0.25, 0.25, ALU.mult, ALU.add), reads=[Bcols], writes=[Bcols])
        S.op("dve", lambda e: e.tensor_scalar(lncol[:, :, 1], lncol[:, :, 1], 0.25, 0.75, ALU.mult, ALU.add), reads=[Bcols], writes=[Bcols])

        ensure_w(W_ORDER[2])
        def late_consts():
            def ld_consts(e, sem):
                def d(o, i):
                    e.dma_start(out=o, in_=i).then_inc(sem, 16)
                d(maskP[:], c_maskP)
                d(maskS[:], c_maskS)
                d(BDf[:], c_BDf)
                d(resetS[:], c_resetS)
                d(R8[:], c_R8)
                d(mcol[:], c_mcol)
                d(gf_bc[:], fin_g.partition_broadcast(128))
                d(g_bc[:], ln_g.partition_broadcast(128))
                d(b_bc[:], ln_b.partition_broadcast(128))
                d(bsrow, b_s.rearrange("h t -> (h t)").unsqueeze(0))
                d(maskPf_sc, c_maskPf)
                d(res[0][:, 0:512].rearrange("p (h s) -> p h s", h=4), w_s.rearrange("h t s -> t h s"))
            NCONST = 12
            S.dma("sp", sem_c, ld_consts, writes=[Bc2, Bres[0], Bres[1], BS, BKhT[0]], n=NCONST)


        def setup_wprep():
            Wt = res[0][:, 0:512]
            WcT32 = res[0][:, 512:1024]
            WbdT32 = res[1][:, 0:512]
            maskPf = maskPf_sc
            beta_bc = b_bc

            def w_tr(e):
                last = None
                for h in range(4):
                    last = e.transpose(banks[3][:, 128 * h:128 * h + 128], Wt[:, 128 * h:128 * h + 128], idf[:])
                return last
            S.op("pe", w_tr, reads=[Bconst, Bc2, Bc3, Bidf, Bres[0]], writes=[BK[3]])
            for h in range(4):
                S.op("dve", lambda e, h=h: e.tensor_tensor(WcT32[:, 128 * h:128 * h + 128], banks[3][:, 128 * h:128 * h + 128], maskPf, ALU.mult),
                     reads=[BK[3], BKhT[0]], writes=[Bres[0]])
            S.op("dve", lambda e: e.tensor_copy(WcTb[:], WcT32), reads=[Bres[0]], writes=[BW])

            def w_tile(e):
                last = None
                for h in range(4):
                    last = e.matmul(banks[4][:, 128 * h:128 * h + 128], Z8[:, h, :], R8[:], start=True, stop=True)
                return last
            S.op("pe", w_tile, reads=[Bcols, Bconst, Bc2, Bc3, BW], writes=[BK[4]])
            for h in range(4):
                S.op("dve", lambda e, h=h: e.tensor_tensor(WbdT32[:, 128 * h:128 * h + 128], banks[4][:, 128 * h:128 * h + 128], BDf[:], ALU.mult),
                     reads=[BK[4], Bconst, Bc2, Bc3], writes=[Bres[1]])
            S.op("dve", lambda e: e.tensor_copy(WbdTb[:], WbdT32), reads=[Bres[1]], writes=[BW])

            def c_mm(e, Wsrc, brow, bank):
                last = None
                for h in range(4):
                    o = bank[:, 128 * h:128 * h + 128]
                    e.matmul(o, beta_bc[:, 128 * h:128 * h + 128], Wsrc[:, 128 * h:128 * h + 128], start=True, stop=False)
                    last = e.matmul(o, ones1[:], brow[:, 128 * h:128 * h + 128], start=False, stop=True)
                return last
            S.op("pe", lambda e: c_mm(e, WcT32, bsrow, banks[5]), reads=[Bres[0], Bconst, Bc2, Bc3, BS, Bmh], writes=[BK[5]])
            S.op("act", lambda e: e.activation(C_P[:], banks[5][:], AF.Copy), reads=[BK[5]], writes=[BW])
            S.op("pe", lambda e: c_mm(e, WbdT32, bsrowS, banks[6]), reads=[Bres[1], Bconst, Bc2, Bc3, Bmh], writes=[BK[6]])
            S.op("act", lambda e: e.activation(C_S[:], banks[6][:], AF.Copy), reads=[BK[6]], writes=[BW])
            S.op("pool", lambda e: e.memset(Sst[:], 0.0), writes=[BS])


        def flagged(g, flag):
            for _ in g:
                yield
            flag["done"] = True

        def delayed(g, k):
            for _ in range(k):
                yield
            for _ in g:
                yield

        def after(flag, g):
            while not flag.get("done"):
                yield
            for _ in g:
                yield

        prev_hb = [None]

        def h_round(tiles, final=False):
            gens = []
            if prev_hb[0] is not None:
                gens.append(gen_Hb(prev_hb[0]))
            flags = [dict() for _ in tiles]
            for k, n in enumerate(tiles):
                g = flagged(gen_Hf(n, flags[k]), flags[k])
                gens.append(delayed(g, 2 * k) if k else g)
            for k, n in enumerate(tiles if final else tiles[:-1]):
                gens.append(after(flags[k], gen_Hb(n)))
            prev_hb[0] = None if final else tiles[-1]
            return igen(*gens)

        for r, m in enumerate(MT_ORDER):
            nxt_tiles = tiles_of(MT_ORDER[r + 1]) if r + 1 < len(MT_ORDER) else []
            gX = chain(*[gen_xn(n) for n in nxt_tiles]) if nxt_tiles else None
            gH = h_round(tiles_of(MT_ORDER[r - 1])) if r >= 1 else None
            interleave_w([(gen_proj(m, first=(r == 0)), 1.0), (gH, RATE_H), (gX, RATE_X)])
            if r == 0:
                late_consts()
                setup_wprep()
                ensure_w("out")
        run(h_round(tiles_of(MT_ORDER[-1]), final=True))
        flush_stores()
        S.final_wait(STQ, out_events)
        S.emit_all()
    return nc


_CACHE = {}
_DBG = {}


def _consts():
    p = np.arange(128)
    tri = (p[:, None] <= p[None, :])
    same = (p[:, None] // 8 == p[None, :] // 8) & (p[:, None] % 8 <= p[None, :] % 8)
    c = {}
    c["c_idb"] = np.eye(128, dtype=np.float32).astype(ml_dtypes.bfloat16)
    c["c_idf"] = np.eye(128, dtype=np.float32)
    c["c_maskP"] = np.ascontiguousarray(np.tile(tri.astype(np.uint8), (1, 4)))
    c["c_maskS"] = np.ascontiguousarray(np.tile(same.astype(np.uint8), (1, 4)))
    c["c_maskPf"] = tri.astype(np.float32)
    c["c_BDf"] = same.astype(np.float32)
    rp = np.ones((128, 1024), np.float32)
    rp[:, ::128] = 0
    c["c_resetP"] = rp.astype(ml_dtypes.bfloat16)
    rs = np.ones((128, 128), np.float32)
    rs[:, ::8] = 0
    c["c_resetS"] = rs.astype(ml_dtypes.bfloat16)
    c["c_R8"] = (np.arange(8)[:, None] == (p[None, :] % 8)).astype(np.float32)
    c["c_mcol"] = (p[:, None] // 8 == np.arange(16)[None, :]).astype(np.float32)
    return c


def kernel(x_prompt, x_sample, state_hgrn, norm_g, w_in, lb_logits, hgrn_norm_g,
           sgu_ln_g, sgu_ln_b, w_s, b_s, w_out, final_norm_g):
    f32 = lambda a: np.ascontiguousarray(np.asarray(a, dtype=np.float32))
    x_prompt, x_sample, state_hgrn = f32(x_prompt), f32(x_sample), f32(state_hgrn)
    if "nc" not in _CACHE:
        _CACHE["nc"] = build_nc()
    nc = _CACHE["nc"]
    consts = _consts()
    shared = {
        "norm_g": f32(norm_g).reshape(D), "w_in": f32(w_in).reshape(D, DIN), "lb_logits": f32(lb_logits).reshape(2, 512),
        "hgrn_g": f32(hgrn_norm_g).reshape(512), "ln_g": f32(sgu_ln_g).reshape(512), "ln_b": f32(sgu_ln_b).reshape(512),
        "w_s": f32(w_s).reshape(4, 128, 128), "b_s": f32(b_s).reshape(4, 128), "w_out": f32(w_out).reshape(D, D),
        "fin_g": f32(final_norm_g).reshape(D),
    }
    shared.update(consts)
    in_maps = []
    for b in range(NCORES):
        mp = dict(shared)
        mp["xp"] = x_prompt[b]
        mp["xs"] = x_sample[16 * b:16 * b + 16].reshape(128, D)
        mp["s0"] = state_hgrn[0, 16 * b:16 * b + 16]
        in_maps.append(mp)
    resr = run_bass_kernel_spmd(nc, in_maps, core_ids=list(range(NCORES)))
    r = resr.results
    g = lambda k, b: np.asarray(r[b][k], dtype=np.float32)
    y_prompt = np.stack([g("yp", b) for b in range(NCORES)], 0)
    y_sample = np.concatenate([g("ys", b).reshape(16, 8, D) for b in range(NCORES)], 0)
    st_p = np.stack([g("sp_o", b) for b in range(NCORES)], 0)[None]
    st_s = np.concatenate([g("ss_o", b) for b in range(NCORES)], 0)[None]
    v_p = np.stack([g("vp_o", b).reshape(128, 4, 128) for b in range(NCORES)], 0)[None]
    v_s = np.concatenate([g("vs_o", b).reshape(16, 8, 4, 128) for b in range(NCORES)], 0)[None]
    return (y_prompt, y_sample, st_p, st_s, v_p, v_s)
```

```python
import numpy as np
import ml_dtypes
from contextlib import ExitStack
import concourse.bass as bass
import concourse.mybir as mybir
from concourse.bass_utils import run_bass_kernel_spmd

F32 = mybir.dt.float32
BF16 = mybir.dt.bfloat16
U8 = mybir.dt.uint8
AF = mybir.ActivationFunctionType
ALU = mybir.AluOpType

EPS = 1e-6
NCORES = 8
SEQ = 2048
NTILE_P = SEQ // 128
NMT_P = NTILE_P // 2
D = 1024
DIN = 3584
G_Q, G_F, G_I, G_ZA, G_U, G_V, G_ZB = range(7)


class Buf:
    __slots__ = ("name", "last_w", "readers")

    def __init__(self, name):
        self.name = name
        self.last_w = None
        self.readers = []


class Sched:
    ENGS = ("pe", "act", "dve", "pool", "sp")

    def __init__(self, nc, es):
        self.nc = nc
        self.es = es
        self.lists = {e: [] for e in self.ENGS}
        self.sems = {}
        self.count = {}
        self.seen = {e: {} for e in self.ENGS}
        for e in ("pe", "act", "dve", "pool"):
            self._mksem("eng_" + e)

    def _mksem(self, key):
        self.sems[key] = self.es.enter_context(self.nc.semaphore(key))
        self.count[key] = 0
        return key

    def dmasem(self, key):
        return self._mksem("dma_" + key)

    def _deps(self, eng, reads, writes):
        need = {}

        def add(ev):
            if ev is None:
                return
            k, v = ev
            if need.get(k, 0) < v:
                need[k] = v
        own = "eng_" + eng
        for b in reads:
            add(b.last_w)
        for b in writes:
            add(b.last_w)
            for r in b.readers:
                add(r)
        out = []
        for k, v in need.items():
            if self.seen[eng].get(k, 0) >= v:
                continue
            self.seen[eng][k] = v
            out.append((k, v))
        return out

    def _record(self, ev, reads, writes):
        for b in writes:
            b.last_w = ev
            b.readers = []
        for b in reads:
            if b not in writes:
                b.readers.append(ev)

    def op(self, eng, fn, reads=(), writes=()):
        waits = self._deps(eng, reads, writes)
        key = "eng_" + eng
        self.count[key] += 1
        val = self.count[key]
        sems = self.sems

        def emit(e, waits=waits, fn=fn, key=key):
            for k, v in waits:
                e.wait_ge(sems[k], v)
            inst = fn(e)
            inst.then_inc(sems[key], 1)
        self.lists[eng].append(emit)
        ev = (key, val)
        self._record(ev, reads, writes)
        return ev

    def dma(self, queue, semkey, fn, reads=(), writes=(), n=1):
        waits = self._deps(queue, reads, writes)
        self.count[semkey] += 16 * n
        val = self.count[semkey]
        sems = self.sems

        def emit(e, waits=waits, fn=fn, semkey=semkey):
            for k, v in waits:
                e.wait_ge(sems[k], v)
            fn(e, sems[semkey])
        self.lists[queue].append(emit)
        ev = (semkey, val)
        self._record(ev, reads, writes)
        return ev

    def final_wait(self, eng, events):
        sems = self.sems
        best = {}
        for k, v in events:
            best[k] = max(best.get(k, 0), v)

        def emit(e):
            for k, v in best.items():
                e.wait_ge(sems[k], v)
        self.lists[eng].append(emit)

    def emit_all(self):
        lists = self.lists
        with self.nc.Block() as block:
            @block.tensor
            def _(e):
                for f in lists["pe"]:
                    f(e)

            @block.scalar
            def _(e):
                for f in lists["act"]:
                    f(e)

            @block.vector
            def _(e):
                for f in lists["dve"]:
                    f(e)

            @block.gpsimd
            def _(e):
                for f in lists["pool"]:
                    f(e)

            @block.sync
            def _(e):
                for f in lists["sp"]:
                    f(e)


def interleave(*gens):
    gens = [g for g in gens if g is not None]
    alive = list(gens)
    while alive:
        nxt = []
        for g in alive:
            try:
                next(g)
                nxt.append(g)
            except StopIteration:
                pass
        alive = nxt


def interleave_w(pairs):
    st = [[g, float(r), 0.0] for g, r in pairs if g is not None]
    while st:
        nxt = []
        for ent in st:
            g, r, acc = ent
            acc += r
            dead = False
            while acc >= 1.0 - 1e-9:
                acc -= 1.0
                try:
                    next(g)
                except StopIteration:
                    dead = True
                    break
            ent[2] = acc
            if not dead:
                nxt.append(ent)
        st = nxt


def chain(*gens):
    for g in gens:
        if g is None:
            continue
        for _ in g:
            yield


def build_nc():
    nc = bass.Bass("TRN2", target_bir_lowering=False)

    def din(name, shape, dt=F32):
        return nc.dram_tensor(name, list(shape), dt, kind="ExternalInput").ap()

    def dout(name, shape, dt=F32):
        return nc.dram_tensor(name, list(shape), dt, kind="ExternalOutput").ap()

    xp = din("xp", [SEQ, D])
    xs = din("xs", [128, D])
    s0 = din("s0", [16, 4, 128, 128])
    norm_g = din("norm_g", [D])
    w_in = din("w_in", [D, DIN])
    lb_logits = din("lb_logits", [2, 512])
    hgrn_g = din("hgrn_g", [512])
    ln_g = din("ln_g", [512])
    ln_b = din("ln_b", [512])
    w_s = din("w_s", [4, 128, 128])
    b_s = din("b_s", [4, 128])
    w_out = din("w_out", [D, D])
    fin_g = din("fin_g", [D])
    c_idb = din("c_idb", [128, 128], BF16)
    c_idf = din("c_idf", [128, 128])
    c_maskP = din("c_maskP", [128, 512], U8)
    c_maskS = din("c_maskS", [128, 512], U8)
    c_maskPf = din("c_maskPf", [128, 128])
    c_BDf = din("c_BDf", [128, 128])
    c_resetP = din("c_resetP", [128, 1024], BF16)
    c_resetS = din("c_resetS", [128, 128], BF16)
    c_R8 = din("c_R8", [8, 128])
    c_mcol = din("c_mcol", [128, 16])

    yp = dout("yp", [SEQ, D])
    ys = dout("ys", [128, D])
    sp_o = dout("sp_o", [4, 128, 128])
    ss_o = dout("ss_o", [16, 4, 128, 128])
    vp_o = dout("vp_o", [128, 512])
    vs_o = dout("vs_o", [128, 512])

    with ExitStack() as es:
        S = Sched(nc, es)
        total = [0]
        dsz = {F32: 4, BF16: 2, U8: 1}

        def sb(name, shape, dt=F32):
            n = 1
            for s_ in shape[1:]:
                n *= s_
            total[0] += n * dsz[dt]
            return es.enter_context(nc.sbuf_tensor(name, list(shape), dt))

        banks = [es.enter_context(nc.psum_tensor("bank%d" % i, [128, 512], F32)) for i in range(8)]
        BK = [Buf("bank%d" % i) for i in range(8)]
        BKH = [[Buf("bank%d_h%d" % (i, h)) for h in range(2)] for i in range(8)]

        w_in_sb = sb("w_in_sb", [128, 8, DIN], BF16)
        w_out_sb = sb("w_out_sb", [128, 8, D], BF16)
        Bw_in = [[Buf("w_in_%d_%d" % (g, c)) for c in range(8)] for g in range(7)]
        Bw_out = [[Buf("w_out_%d_%d" % (hf, c)) for c in range(8)] for hf in range(2)]

        xin = [sb("xin%d" % i, [128, D]) for i in range(2)]
        Bxin = [Buf("xin%d" % i) for i in range(2)]
        xres = [sb("xres%d" % i, [128, D]) for i in range(2)]
        Bxres = [Buf("xres%d" % i) for i in range(2)]
        xn = [sb("xn%d" % i, [128, D], BF16) for i in range(1)]
        Bxn = [Buf("xn%d" % i) for i in range(1)]
        hT = [sb("hT%d" % i, [128, 8, 256], BF16) for i in range(2)]
        BhT = [[Buf("hT%d_%d" % (i, j)) for j in range(2)] for i in range(2)]

        Tt = sb("Tt", [128, 4, 256])
        kk = sb("kk", [128, 4, 256], BF16)
        bb = sb("bb", [128, 4, 256])
        E1 = sb("E1", [128, 4, 256])
        qs = sb("qs", [128, 4, 256], BF16)
        BT = [Buf("T%d" % h) for h in range(4)]
        Bkk = [Buf("kk%d" % h) for h in range(4)]
        Bbb = [Buf("bb%d" % i) for i in range(8)]
        BE1 = [Buf("E1_%d" % i) for i in range(8)]
        Bqs = [Buf("qs%d" % i) for i in range(2)]
        rcol = sb("rcol", [128, 4, 2])
        nrcol = sb("nrcol", [128, 4, 2])
        Brcol = Buf("rcol")
        zero_col = sb("zero_col", [128, 1])
        dcolS = sb("dcolS", [128, 4, 16])
        BdcolS = Buf("dcolS")

        def dbl(name, shape, dt):
            return [sb("%s%d" % (name, i), shape, dt) for i in range(2)]
        Qt = dbl("Qt", [128, 4, 256], BF16)
        Kt = dbl("Kt", [128, 4, 256], BF16)
        Kh = dbl("Kh", [128, 4, 256], BF16)
        sza = dbl("sza", [128, 4, 256], BF16)
        gu = dbl("gu", [128, 4, 256], BF16)
        szb = dbl("szb", [128, 4, 256], BF16)
        ercol = dbl("ercol", [128, 4, 2], F32)
        dcol = dbl("dcol", [128, 4, 2], F32)
        BQt = [Buf("Qt%d" % i) for i in range(2)]
        BKt = [Buf("Kt%d" % i) for i in range(2)]
        BKhat = [Buf("Kh%d" % i) for i in range(2)]
        Bsza = [[Buf("sza%d_%d" % (i, j)) for j in range(2)] for i in range(2)]
        Bgu = [[Buf("gu%d_%d" % (i, j)) for j in range(2)] for i in range(2)]
        Bszb = [[Buf("szb%d_%d" % (i, j)) for j in range(2)] for i in range(2)]
        Bercol = [Buf("ercol%d" % i) for i in range(2)]
        Bdcol = [Buf("dcol%d" % i) for i in range(2)]
        BE1keep = [Buf("E1keep%d" % i) for i in range(2)]

        Vt = [sb("Vt%d" % i, [128, 512], BF16) for i in range(4)]
        BVt = [Buf("Vt%d" % i) for i in range(4)]
        vhat = [sb("vhat%d" % i, [128, 512], BF16) for i in range(4)]
        Bvhat = [Buf("vhat%d" % i) for i in range(4)]
        gv = [sb("gv%d" % i, [128, 512]) for i in range(2)]
        Bgv = [Buf("gv%d" % i) for i in range(2)]
        stat = [sb("stat%d" % i, [128, 6]) for i in range(2)]
        mv = [sb("mv%d" % i, [128, 2]) for i in range(2)]
        vrs = [sb("vrs%d" % i, [128, 2]) for i in range(2)]
        Bstat = [Buf("stat%d" % i) for i in range(2)]
        Bmv = [Buf("mv%d" % i) for i in range(2)]
        Bvrs = [Buf("vrs%d" % i) for i in range(2)]
        mhalf = sb("mhalf", [128, 4])

        sTm = [sb("sTm%d" % i, [128, 512], BF16) for i in range(3)]
        BsTm = [Buf("sTm%d" % i) for i in range(3)]
        KhT = [sb("KhT%d" % i, [128, 512], BF16) for i in range(2)]
        BKhT = [Buf("KhT%d" % i) for i in range(2)]
        Sst = sb("Sst", [128, 512])
        BS = Buf("S")
        Spb = [sb("Spb%d" % i, [128, 512], BF16) for i in range(2)]
        BSpb = [Buf("Spb%d" % i) for i in range(2)]
        on = [sb("on%d" % i, [128, 512], BF16) for i in range(2)]
        Bon = [Buf("on%d" % i) for i in range(2)]
        sso = [sb("sso%d" % i, [128, 4]) for i in range(2)]
        rso = [sb("rso%d" % i, [128, 4]) for i in range(2)]
        Bsso = [Buf("sso%d" % i) for i in range(2)]
        Brso = [Buf("rso%d" % i) for i in range(2)]
        mixT = [sb("mixT%d" % i, [128, 8, 128], BF16) for i in range(2)]
        BmixA = [Buf("mixA%d" % i) for i in range(2)]
        BmixB = [Buf("mixB%d" % i) for i in range(2)]
        t1 = [sb("t1_%d" % i, [128, 512], BF16) for i in range(2)]
        Bt1 = [Buf("t1_%d" % i) for i in range(2)]
        Pm = [sb("Pm%d" % i, [128, 512], BF16) for i in range(2)]
        BPm = [Buf("Pm%d" % i) for i in range(2)]
        res = [sb("res%d" % i, [128, D]) for i in range(2)]
        Bres = [Buf("res%d" % i) for i in range(2)]
        ssx = [sb("ssx%d" % i, [128, 1]) for i in range(2)]
        rsx = [sb("rsx%d" % i, [128, 1]) for i in range(2)]
        Bssx = [Buf("ssx%d" % i) for i in range(2)]
        Brsx = [Buf("rsx%d" % i) for i in range(2)]
        ssf = [sb("ssf%d" % i, [128, 1]) for i in range(2)]
        rsf = [sb("rsf%d" % i, [128, 1]) for i in range(2)]
        Bssf = [Buf("ssf%d" % i) for i in range(2)]
        Brsf = [Buf("rsf%d" % i) for i in range(2)]

        idb = sb("idb", [128, 128], BF16)
        idf = sb("idf", [128, 128])
        maskP = sb("maskP", [128, 512], U8)
        maskS = sb("maskS", [128, 512], U8)
        BDf = sb("BDf", [128, 128])
        resetP = sb("resetP", [128, 1024], BF16)
        resetS = sb("resetS", [128, 128], BF16)
        R8 = sb("R8", [8, 128])
        mcol = sb("mcol", [128, 16])
        ngc = sb("ngc", [128, 8])
        lbl = sb("lbl", [128, 2, 4])
        lncol = sb("lncol", [128, 4, 4])
        ghn = sb("ghn", [128, 4])
        gln = sb("gln", [128, 4])
        gf_bc = sb("gf_bc", [128, D])
        g_bc = sb("g_bc", [128, 512])
        b_bc = sb("b_bc", [128, 512])
        ones1 = sb("ones1", [1, 128])
        WcTb = sb("WcTb", [128, 512], BF16)
        WbdTb = sb("WbdTb", [128, 512], BF16)
        C_P = sb("C_P", [128, 512])
        C_S = sb("C_S", [128, 512])
        Bconst = Buf("const")
        Bcols = Buf("cols")
        BW = Buf("Wprep")

        bsrow = Sst[0:1, :]
        bsrowS = res[1][0:1, 512:1024]
        Z8 = C_S[0:8, :].rearrange("p (h t) -> p h t", h=4)
        maskPf_sc = KhT[0][:].bitcast(F32)[:, 0:128]
        s0buf = [sb("s0buf%d" % i, [128, 512]) for i in range(3)]
        Bs0buf = [Buf("s0buf%d" % i) for i in range(3)]
        s0b = [sb("s0b%d" % i, [128, 512], BF16) for i in range(2)]
        Bs0b = [Buf("s0b%d" % i) for i in range(2)]
        KhTm = [sb("KhTm%d" % i, [128, 512], BF16) for i in range(2)]
        BKhTm = [Buf("KhTm%d" % i) for i in range(2)]
        oT_sb = s0buf[1]
        BoT = Bs0buf[1]

        assert total[0] < 212800, total[0]

        sem_c = S.dmasem("const")
        sem_c2 = S.dmasem("const2")
        sem_x = [S.dmasem("xin%d" % i) for i in range(2)]
        sem_xr = [S.dmasem("xres%d" % i) for i in range(2)]
        sem_y = [S.dmasem("y%d" % i) for i in range(2)]
        sem_s0 = [S.dmasem("s0_%d" % i) for i in range(3)]
        sem_so = [S.dmasem("so_%d" % i) for i in range(3)]
        sem_misc = S.dmasem("misc_out")
        sem_vo = [S.dmasem("vo%d" % i) for i in range(2)]
        sem_spo = S.dmasem("spo")
        out_events = []
        pend_st = []

        def flush_stores(keep=0):
            while len(pend_st) > keep:
                pend_st.pop(0)()
        STQ = "sp"

        def f32view(t, pat):
            return t[:].rearrange(pat).bitcast(F32)
        wst = [xres[0][:, 0:512], xres[0][:, 512:1024], xres[1][:, 0:512], xres[1][:, 512:1024],
               s0buf[0][:], s0buf[1][:], s0buf[2][:],
               f32view(Qt[1], "p h t -> p (h t)"), f32view(Kt[1], "p h t -> p (h t)"), f32view(Kh[1], "p h t -> p (h t)"),
               f32view(sza[1], "p h t -> p (h t)"), f32view(gu[1], "p h t -> p (h t)"), f32view(szb[1], "p h t -> p (h t)"),
               f32view(mixT[0], "p c t -> p (c t)"), f32view(mixT[1], "p c t -> p (c t)")]
        wst_alias = [[Bxres[0]], [Bxres[0]], [Bxres[1]], [Bxres[1]],
                     [Bs0buf[0]], [Bs0buf[1]], [Bs0buf[2]],
                     [BQt[1]], [BKt[1]], [BKhat[1]], Bsza[1], Bgu[1], Bszb[1],
                     [BmixA[0], BmixB[0]], [BmixA[1], BmixB[1]]]
        NSLOT = len(wst)
        Bwst = [Buf("wst%d" % i) for i in range(NSLOT)]
        sem_w = [S.dmasem("wstg%d" % i) for i in range(NSLOT)]
        wcnt = [0]

        def load_w(src_ap, dst_ap, Bdst, scale_ap):
            k = wcnt[0] % NSLOT
            wcnt[0] += 1
            al = wst_alias[k]
            S.dma("sp", sem_w[k], lambda e, sem, k=k, src_ap=src_ap: e.dma_start(out=wst[k], in_=src_ap).then_inc(sem, 16),
                  writes=[Bwst[k]] + al)
            if wcnt[0] % 2 == 0:
                if scale_ap is None:
                    S.op("dve", lambda e, k=k: e.tensor_copy(dst_ap, wst[k]), reads=[Bwst[k]] + al, writes=[Bdst])
                else:
                    S.op("dve", lambda e, k=k: e.tensor_scalar(dst_ap, wst[k], scale_ap, None, ALU.mult),
                         reads=[Bwst[k], Bngc] + al, writes=[Bdst])
            else:
                if scale_ap is None:
                    S.op("act", lambda e, k=k: e.activation(dst_ap, wst[k], AF.Copy), reads=[Bwst[k]] + al, writes=[Bdst])
                else:
                    S.op("act", lambda e, k=k: e.activation(dst_ap, wst[k], AF.Copy, scale=scale_ap),
                         reads=[Bwst[k], Bngc] + al, writes=[Bdst])

        def gen_load_w_in(groups):
            for g in groups:
                for c in range(8):
                    load_w(w_in[128 * c:128 * c + 128, 512 * g:512 * g + 512], w_in_sb[:, c, 512 * g:512 * g + 512],
                           Bw_in[g][c], ngc[:, c:c + 1])
                    yield

        def gen_load_w_out():
            for hf in range(2):
                for c in range(8):
                    load_w(w_out[128 * c:128 * c + 128, 512 * hf:512 * hf + 512], w_out_sb[:, c, 512 * hf:512 * hf + 512],
                           Bw_out[hf][c], None)
                    yield

        def midx(m):
            return {0: 0, NMT_P: 1}.get(m, m + 1)

        def tidx(n):
            return {0: 0, 1: 1, NTILE_P: 2}.get(n, n + 1)

        def tiles_of(m):
            return [2 * m, 2 * m + 1] if m < NMT_P else [NTILE_P]

        W_ORDER = [G_F, G_Q, G_ZA, G_ZB, G_U, G_V, G_I]
        w_loaded = set()

        def ensure_w(key):
            if key in w_loaded:
                return
            w_loaded.add(key)
            if key == "out":
                for _ in gen_load_w_out():
                    pass
            else:
                for _ in gen_load_w_in([key]):
                    pass

        def tile_rows(n):
            return (xp[128 * n:128 * n + 128, :], yp[128 * n:128 * n + 128, :]) if n < NTILE_P else (xs, ys)

        def pow_rsqrt(dst, src, Bdst, Bsrc, scale, width):
            S.op("dve", lambda e: e.tensor_scalar(dst, src, scale, EPS, ALU.mult, ALU.add), reads=[Bsrc], writes=[Bdst])
            S.op("pool", lambda e: e.tensor_tensor(dst, dst, mhalf[:, 0:width], ALU.pow), reads=[Bdst, Bmh], writes=[Bdst])

        def gen_xn(n, queue="sp"):
            m, j = (n // 2, n % 2) if n < NTILE_P else (NMT_P, 0)
            mb = midx(m) % 2
            xb = 0
            xi = tidx(n) % 2
            src, _ = tile_rows(n)
            S.dma(queue, sem_x[xi], lambda e, sem: e.dma_start(out=xin[xi][:], in_=src).then_inc(sem, 16), writes=[Bxin[xi]])
            yield
            S.op("act", lambda e: e.activation(xn[xb][:], xin[xi][:], AF.Square, accum_out=ssx[xi][:]), reads=[Bxin[xi]], writes=[Bssx[xi], Bxn[xb]])
            S.op("pool", lambda e: e.tensor_scalar(rsx[xi][:], ssx[xi][:], 1.0 / D, EPS, ALU.mult, ALU.add), reads=[Bssx[xi]], writes=[Brsx[xi]])
            S.op("pool", lambda e: e.tensor_tensor(rsx[xi][:], rsx[xi][:], mhalf[:, 0:1], ALU.pow), reads=[Brsx[xi], Bmh], writes=[Brsx[xi]])
            if tidx(n) < 2:
                S.op("dve", lambda e: e.tensor_scalar(xn[xb][:], xin[xi][:], rsx[xi][:], None, ALU.mult), reads=[Bxin[xi], Brsx[xi]], writes=[Bxn[xb]])
            else:
                S.op("pool", lambda e: e.tensor_scalar(xn[xb][:], xin[xi][:], rsx[xi][:], 1.0, ALU.mult, ALU.mult), reads=[Bxin[xi], Brsx[xi]], writes=[Bxn[xb]])
            yield
            yield
            yield
            xbk = next_bank()
            pbank = banks[xbk][:].bitcast(BF16)

            def tr(e):
                last = None
                for c in range(8):
                    last = e.transpose(pbank[:, 128 * c:128 * c + 128], xn[xb][:, 128 * c:128 * c + 128], idb[:])
                return last
            S.op("pe", tr, reads=[Bxn[xb], Bidb], writes=[BK[xbk]])
            dst = hT[mb][:, :, 128 * j:128 * j + 128]
            S.op("dve", lambda e: e.tensor_copy(dst, pbank.rearrange("p (c t) -> p c t", c=8)), reads=[BK[xbk]], writes=[BhT[mb][j]])
            yield

        proj_rot = [0]

        def next_bank():
            k = proj_rot[0] % 3
            proj_rot[0] += 1
            return k

        def fm_view(t, NT):
            return t[:].rearrange("p h t -> p (h t)")[:, 0:4 * NT].rearrange("p (h t) -> p h t", h=4)

        def gen_proj(m, first=False):
            NT = 256 if m < NMT_P else 128
            ntl = NT // 128
            mb = midx(m) % 2
            hTm = hT[mb]
            BhTm = [BhT[mb][j] for j in range(ntl)]
            tiles = tiles_of(m)
            wpos = [0]

            def prefetch_w(k):
                if first:
                    for g in W_ORDER[wpos[0]:wpos[0] + k]:
                        ensure_w(g)
                    wpos[0] += k
            prefetch_w(3)

            def fm_group(g, evac):
                for bp in range(2):
                    bk = next_bank()

                    def mm(e, bp=bp, bk=bk):
                        last = None
                        for bl in range(2):
                            col0 = 512 * g + 128 * (2 * bp + bl)
                            for c in range(8):
                                last = e.matmul(banks[bk][:, bl * NT:(bl + 1) * NT], w_in_sb[:, c, col0:col0 + 128], hTm[:, c, 0:NT],
                                                start=(c == 0), stop=(c == 7))
                        return last
                    S.op("pe", mm, reads=BhTm + Bw_in[g], writes=[BK[bk]])
                    evac(bp, banks[bk][:, 0:2 * NT], BK[bk])
                    if bp == 0:
                        prefetch_w(1)
                    yield

            def dst2(t, bp):
                return fm_view(t, NT)[:, 2 * bp:2 * bp + 2, :].rearrange("p a t -> p (a t)")

            Tv, kkv, bbv, E1v, qsv = (fm_view(t, NT) for t in (Tt, kk, bb, E1, qs))
            prompt = m < NMT_P
            fT = (lambda h: [BT[h]]) if prompt else (lambda h: BT)
            fkk = (lambda h: [Bkk[h]]) if prompt else (lambda h: Bkk)
            fqs = (lambda bp: [Bqs[bp]]) if prompt else (lambda bp: Bqs)
            fza = (lambda bp: [Bsza[mb][bp]]) if prompt else (lambda bp: Bsza[mb])
            fzb = (lambda bp: [Bszb[mb][bp]]) if prompt else (lambda bp: Bszb[mb])
            fgu = (lambda bp: [Bgu[mb][bp]]) if prompt else (lambda bp: Bgu[mb])
            G = 128 if prompt else 8
            NG = NT // G

            def ev_f(bp, src, Bsrc):
                S.op("act", lambda e: e.activation(dst2(Tt, bp), src, AF.Tanh, scale=0.5), reads=[Bsrc], writes=fT(2 * bp) + fT(2 * bp + 1))
            yield from fm_group(G_F, ev_f)

            for h in range(4):
                S.op("dve", lambda e, h=h: e.tensor_scalar(kkv[:, h, :], Tv[:, h, :], lncol[:, h, 2:3], lncol[:, h, 3:4], ALU.mult, ALU.add),
                     reads=fT(h) + [Bcols], writes=fkk(h))
            for h in range(4):
                S.op("act", lambda e, h=h: e.activation(Tv[:, h, :], Tv[:, h, :], AF.Ln, scale=lncol[:, h, 0:1], bias=lncol[:, h, 1:2]),
                     reads=fT(h) + fkk(h) + [Bcols], writes=fT(h))

            def ev_q(bp, src, Bsrc):
                S.op("act", lambda e: e.activation(dst2(qs, bp), src, AF.Silu), reads=[Bsrc], writes=fqs(bp))
            yield from fm_group(G_Q, ev_q)

            if prompt:
                S.op("dve", lambda e: e.tensor_tensor_scan(bb[:].rearrange("p h t -> p (h t)"), resetP[:], Tt[:].rearrange("p h t -> p (h t)"),
                                                          0.0, ALU.mult, ALU.add), reads=BT + [Bconst], writes=Bbb)
                bb4 = bb[:].rearrange("p h (j t) -> p h j t", j=2)
                S.op("dve", lambda e: e.tensor_copy(rcol[:], bb4[:, :, :, 63]), reads=Bbb, writes=[Brcol])
            else:
                for h in range(4):
                    S.op("dve", lambda e, h=h: e.tensor_tensor_scan(bbv[:, h, :], resetS[:], Tv[:, h, :], 0.0, ALU.mult, ALU.add),
                         reads=BT + [Bconst, Bc2, Bc3], writes=Bbb)

            def ev_za(bp, src, Bsrc):
                S.op("act", lambda e: e.activation(dst2(sza[mb], bp), src, AF.Silu), reads=[Bsrc], writes=fza(bp))
            yield from fm_group(G_ZA, ev_za)

            def ev_zb(bp, src, Bsrc):
                S.op("act", lambda e: e.activation(dst2(szb[mb], bp), src, AF.Silu), reads=[Bsrc], writes=fzb(bp))
            yield from fm_group(G_ZB, ev_zb)

            if prompt:
                S.op("act", lambda e: e.activation(ercol[mb][:], rcol[:], AF.Exp), reads=[Brcol], writes=[Bercol[mb]])
                bb4v = bb[:].rearrange("p h (j t) -> p h j t", j=2)
                E14v = E1[:].rearrange("p h (j t) -> p h j t", j=2)
                S.op("dve", lambda e: e.tensor_tensor(E14v, bb4v, rcol[:].unsqueeze(3).to_broadcast([128, 4, 2, 128]), ALU.subtract),
                     reads=Bbb + [Brcol], writes=BE1)
                bblast = bb4v[:, :, :, G - 1:G].rearrange("p h j o -> p h (j o)")
                S.op("act", lambda e: e.activation(dcol[mb][:], bblast, AF.Exp), reads=Bbb, writes=[Bdcol[mb]])
                S.op("act", lambda e: e.activation(bb[:].rearrange("p h t -> p (h t)"), E1[:].rearrange("p h t -> p (h t)"), AF.Exp, scale=-1.0),
                     reads=BE1 + [Bdcol[mb]], writes=Bbb)
                S.op("act", lambda e: e.activation(E1[:].rearrange("p h t -> p (h t)"), E1[:].rearrange("p h t -> p (h t)"), AF.Exp),
                     reads=BE1 + Bbb, writes=BE1)
            else:
                S.op("act", lambda e: e.activation(E1v.rearrange("p h t -> p (h t)"), bbv.rearrange("p h t -> p (h t)"), AF.Exp),
                     reads=Bbb, writes=BE1)
            f2 = lambda a: a.rearrange("p h t -> p (h t)")
            E1g = E1v.rearrange("p h (j t) -> p h j t", t=G)
            if not prompt:
                S.op("dve", lambda e: e.tensor_copy(dcolS[:], E1g[:, :, :, G - 1:G].rearrange("p h j o -> p h (j o)")), reads=BE1, writes=[BdcolS])
            Qv, Kv, Khv = fm_view(Qt[mb], NT), fm_view(Kt[mb], NT), fm_view(Kh[mb], NT)
            S.op("pool", lambda e: e.tensor_tensor(f2(Qv), f2(qsv), f2(E1v), ALU.mult), reads=Bqs + BE1, writes=[BQt[mb]])

            if not prompt:
                S.op("act", lambda e: e.activation(bbv.rearrange("p h t -> p (h t)"), bbv.rearrange("p h t -> p (h t)"), AF.Exp, scale=-1.0),
                     reads=Bbb, writes=Bbb)
            S.op("pool", lambda e: e.tensor_tensor(f2(Kv), f2(kkv), f2(bbv), ALU.mult), reads=Bkk + Bbb, writes=[BKt[mb]])
            S.op("pool", lambda e: e.tensor_tensor(Khv.rearrange("p h (j t) -> p h j t", t=G), Kv.rearrange("p h (j t) -> p h j t", t=G),
                                                  E1g[:, :, :, G - 1:G].to_broadcast([128, 4, NG, G]), ALU.mult),
                 reads=[BKt[mb]] + BE1, writes=[BKhat[mb]])

            def ev_u(bp, src, Bsrc):
                S.op("act", lambda e: e.activation(dst2(gu[mb], bp), src, AF.Gelu_apprx_tanh), reads=[Bsrc], writes=fgu(bp))
            yield from fm_group(G_U, ev_u)

            for j, n in enumerate(tiles):
                bk = next_bank()
                vb = tidx(n) % 4
                gb = tidx(n) % 2

                def mmv(e, j=j, bk=bk):
                    last = None
                    for c in range(8):
                        last = e.matmul(banks[bk][:], hTm[:, c, 128 * j:128 * j + 128], w_in_sb[:, c, 512 * G_V:512 * G_V + 512],
                                        start=(c == 0), stop=(c == 7))
                    return last
                S.op("pe", mmv, reads=[BhTm[j]] + Bw_in[G_V], writes=[BK[bk]])
                S.op("act", lambda e, bk=bk, gb=gb: e.activation(gv[gb][:], banks[bk][:], AF.Gelu_apprx_tanh), reads=[BK[bk]], writes=[Bgv[gb]])
                yield
                S.op("dve", lambda e, gb=gb: e.bn_stats(stat[gb][:], gv[gb][:]), reads=[Bgv[gb]], writes=[Bstat[gb]])
                S.op("dve", lambda e, gb=gb: e.bn_aggr(mv[gb][:], stat[gb][:]), reads=[Bstat[gb]], writes=[Bmv[gb]])
                S.op("dve", lambda e, gb=gb: e.tensor_scalar(vrs[gb][:, 0:1], mv[gb][:, 1:2], EPS, None, ALU.add), reads=[Bmv[gb]], writes=[Bvrs[gb]])
                S.op("pool", lambda e, gb=gb: e.tensor_tensor(vrs[gb][:, 0:1], vrs[gb][:, 0:1], mhalf[:, 0:1], ALU.pow),
                     reads=[Bvrs[gb], Bmh], writes=[Bvrs[gb]])
                yield
                S.op("dve", lambda e, gb=gb: e.scalar_tensor_tensor(vrs[gb][:, 1:2], mv[gb][:, 0:1], -1.0, vrs[gb][:, 0:1], ALU.mult, ALU.mult),
                     reads=[Bmv[gb], Bvrs[gb]], writes=[Bvrs[gb]])
                yield
                S.op("dve", lambda e, gb=gb, vb=vb: e.tensor_scalar(vhat[vb][:], gv[gb][:], vrs[gb][:, 0:1], vrs[gb][:, 1:2], ALU.mult, ALU.add),
                     reads=[Bgv[gb], Bvrs[gb]], writes=[Bvhat[vb]])
                if n == NTILE_P - 1 or n == NTILE_P:
                    dsto = vp_o if n == NTILE_P - 1 else vs_o
                    S.op("act", lambda e, gb=gb: e.activation(gv[gb][:], gv[gb][:], AF.Identity, scale=vrs[gb][:, 0:1], bias=vrs[gb][:, 1:2]),
                         reads=[Bgv[gb], Bvrs[gb], Bvhat[vb]], writes=[Bgv[gb]])
                    S.op("dve", lambda e, gb=gb: e.tensor_tensor(gv[gb][:], gv[gb][:], g_bc[:], ALU.mult), reads=[Bgv[gb], Bconst, Bc2, Bc3], writes=[Bgv[gb]])
                    S.op("dve", lambda e, gb=gb: e.tensor_tensor(gv[gb][:], gv[gb][:], b_bc[:], ALU.add), reads=[Bgv[gb], Bconst, Bc2, Bc3], writes=[Bgv[gb]])
                    out_events.append(S.dma(STQ, sem_vo[n - (NTILE_P - 1)], lambda e, sem, dsto=dsto, gb=gb: e.dma_start(out=dsto, in_=gv[gb][:]).then_inc(sem, 16),
                                            reads=[Bgv[gb]]))
                if j == 0:
                    prefetch_w(1)
                yield

            for j, n in enumerate(tiles):
                bk = next_bank()
                vb = tidx(n) % 4

                def mmi(e, j=j, bk=bk):
                    last = None
                    for c in range(8):
                        last = e.matmul(banks[bk][:], hTm[:, c, 128 * j:128 * j + 128], w_in_sb[:, c, 512 * G_I:512 * G_I + 512],
                                        start=(c == 0), stop=(c == 7))
                    return last
                S.op("pe", mmi, reads=[BhTm[j]] + Bw_in[G_I], writes=[BK[bk]])
                S.op("dve", lambda e, bk=bk, vb=vb: e.tensor_copy(Vt[vb][:], banks[bk][:]), reads=[BK[bk]], writes=[BVt[vb]])
                yield
            if first:
                for g in W_ORDER:
                    ensure_w(g)

        hcount = [0]
        htb = {}

        def gen_Hf(n, flag=None):
            sample = (n == NTILE_P)
            m, j = (n // 2, n % 2) if not sample else (NMT_P, 0)
            mb = midx(m) % 2
            NT = 128 if sample else 256
            tb = hcount[0] % 2
            first_h = hcount[0] < 2
            hcount[0] += 1
            htb[n] = tb
            ob = 4 if tb == 0 else 6
            vb = tidx(n) % 4
            t0 = 128 * j
            Qv, Kv, Khv = fm_view(Qt[mb], NT), fm_view(Kt[mb], NT), fm_view(Kh[mb], NT)
            guv, szbv = fm_view(gu[mb], NT), fm_view(szb[mb], NT)
            sl = slice(t0, t0 + 128)
            src, dsty = tile_rows(n)
            hs = lambda h: slice(128 * h, 128 * h + 128)

            S.dma("sp", sem_xr[tb], lambda e, sem: e.dma_start(out=xres[tb][:], in_=src).then_inc(sem, 16),
                  writes=[Bxres[tb]])

            if (not sample) and j == 0:
                S.op("dve", lambda e: e.tensor_tensor(Spb[n % 2][:].rearrange("k (h v) -> k h v", h=4), Sst[:].rearrange("k (h v) -> k h v", h=4),
                                                      ercol[mb][:, :, 0:1].to_broadcast([128, 4, 128]), ALU.mult),
                     reads=[BS, Bercol[mb]], writes=[BSpb[n % 2]])
            def mm_s(e):
                last = None
                for h in range(4):
                    e.matmul(banks[3][0:64, 128 * h:128 * h + 128], Kv[:, h, t0:t0 + 64], Qv[:, h, sl], start=True, stop=True)
                    last = e.matmul(banks[3][64:128, 128 * h + 64:128 * h + 128], Kv[:, h, t0 + 64:t0 + 128], Qv[:, h, t0 + 64:t0 + 128],
                                    start=True, stop=True)
                return last
            S.op("pe", mm_s, reads=[BKt[mb], BQt[mb]], writes=[BK[3]])
            sti = 2 if sample else tb
            S.op("dve", lambda e: e.copy_predicated(sTm[sti][:], (maskS if sample else maskP)[:], banks[3][:]),
                 reads=[BK[3], Bconst, Bc2, Bc3], writes=[BsTm[sti]])
            pb = banks[5][:, 256:512].bitcast(BF16)

            def tr_k(e):
                last = None
                for h in range(4):
                    last = e.transpose(pb[:, hs(h)], Khv[:, h, sl], idb[:])
                return last
            S.op("pe", tr_k, reads=[BKhat[mb], Bidb], writes=[BK[5]])
            S.op("dve", lambda e: e.tensor_copy(KhT[tb][:], pb), reads=[BK[5]], writes=[BKhT[tb]])
            yield

            if not sample:
                sp_i = n % 2

                def emit_sprime(jj, idx):
                    S.op("dve", lambda e: e.tensor_tensor(Spb[idx][:].rearrange("k (h v) -> k h v", h=4), Sst[:].rearrange("k (h v) -> k h v", h=4),
                                                          ercol[mb][:, :, jj:jj + 1].to_broadcast([128, 4, 128]), ALU.mult),
                         reads=[BS, Bercol[mb]], writes=[BSpb[idx]])

                def mm_o(e):
                    last = None
                    for h in range(4):
                        e.matmul(banks[ob][:, hs(h)], Qv[:, h, sl], Spb[sp_i][:, hs(h)], start=True, stop=False)
                        last = e.matmul(banks[ob][:, hs(h)], sTm[tb][:, hs(h)], Vt[vb][:, hs(h)], start=False, stop=True)
                    return last
                S.op("pe", mm_o, reads=[BQt[mb], BSpb[sp_i], BsTm[tb], BVt[vb]], writes=[BK[ob]])

                def mm_u(e):
                    last = None
                    for h in range(4):
                        last = e.matmul(banks[5][:, hs(h)], KhT[tb][:, hs(h)], Vt[vb][:, hs(h)], start=True, stop=True)
                    return last
                S.op("pe", mm_u, reads=[BKhT[tb], BVt[vb]], writes=[BK[5]])
                for h in range(4):
                    S.op("dve", lambda e, h=h: e.scalar_tensor_tensor(Sst[:, hs(h)], Sst[:, hs(h)], dcol[mb][:, h, j:j + 1], banks[5][:, hs(h)],
                                                                      ALU.mult, ALU.add),
                         reads=[BS, Bdcol[mb], BK[5]], writes=[BS])
                if j == 0:
                    emit_sprime(1, 1 - sp_i)
                if n == NTILE_P - 1:
                    out_events.append(S.dma(STQ, sem_spo, lambda e, sem: e.dma_start(
                        out=sp_o.rearrange("h k v -> k h v"), in_=Sst[:].rearrange("k (h v) -> k h v", h=4)).then_inc(sem, 16), reads=[BS]))
                yield
            else:
                def mm_oi(e):
                    last = None
                    for h in range(4):
                        last = e.matmul(banks[ob][:, hs(h)], Vt[vb][:, hs(h)], sTm[2][:, hs(h)], start=(h == 0), stop=False, skip_group_check=True)
                    return last
                S.op("pe", mm_oi, reads=[BsTm[2], BVt[vb]], writes=[BK[ob]])
                so_pend = []
                def ld_s0(q):
                    sbi = q % 3
                    S.dma("sp", sem_s0[sbi], lambda e, sem: e.dma_start(
                        out=s0buf[sbi][:].rearrange("k (h v) -> k h v", h=4), in_=s0[q].rearrange("h k v -> k h v")).then_inc(sem, 16),
                        writes=[Bs0buf[sbi]])
                def prep_q(q):
                    sbi, cbi = q % 3, q % 2
                    S.op("act", lambda e: e.activation(s0b[cbi][:], s0buf[sbi][:], AF.Copy), reads=[Bs0buf[sbi]], writes=[Bs0b[cbi]])
                    S.op("dve", lambda e: e.tensor_scalar(KhTm[cbi][:], KhT[tb][:], mcol[:, q:q + 1], None, ALU.mult),
                         reads=[BKhT[tb], Bconst, Bc2, Bc3], writes=[BKhTm[cbi]])

                def st_s0(q):
                    sbi = q % 3
                    out_events.append(S.dma(STQ, sem_so[sbi], lambda e, sem: e.dma_start(
                        out=ss_o[q].rearrange("h k v -> k h v"), in_=s0buf[sbi][:].rearrange("k (h v) -> k h v", h=4)).then_inc(sem, 16),
                        reads=[Bs0buf[sbi]]))
                ld_s0(0)
                ld_s0(1)
                prep_q(0)
                for q in range(16):
                    sbi, cbi, ubank = q % 3, q % 2, (5 if q % 2 == 0 else 3)
                    if q >= 1:
                        st_s0(q - 1)
                    if q + 2 < 16:
                        ld_s0(q + 2)

                    def mm_q(e, q=q, cbi=cbi):
                        last = None
                        for h in range(4):
                            last = e.matmul(banks[ob][:, 128 * h + 8 * q:128 * h + 8 * q + 8], s0b[cbi][:, hs(h)], Qv[:, h, 8 * q:8 * q + 8],
                                            start=False, stop=(q == 15 and h == 3), skip_group_check=True)
                        return last
                    S.op("pe", mm_q, reads=[Bs0b[cbi], BQt[mb]], writes=[BK[ob]])

                    def mm_uq(e, cbi=cbi, ubank=ubank):
                        last = None
                        for h in range(4):
                            last = e.matmul(banks[ubank][:, hs(h)], KhTm[cbi][:, hs(h)], Vt[vb][:, hs(h)], start=True, stop=True)
                        return last
                    S.op("pe", mm_uq, reads=[BKhTm[cbi], BVt[vb]], writes=[BK[ubank]])
                    if q + 1 < 16:
                        prep_q(q + 1)
                    for h in range(4):
                        S.op("dve", lambda e, h=h, q=q, sbi=sbi, ubank=ubank: e.scalar_tensor_tensor(
                            s0buf[sbi][:, hs(h)], s0buf[sbi][:, hs(h)], dcolS[:, h, q:q + 1], banks[ubank][:, hs(h)], ALU.mult, ALU.add),
                            reads=[Bs0buf[sbi], BK[ubank], BdcolS], writes=[Bs0buf[sbi]])
                    yield
                st_s0(15)
                S.op("act", lambda e: e.activation(oT_sb[:], banks[ob][:], AF.Copy), reads=[BK[ob]], writes=[BoT])

                def tr_o(e):
                    last = None
                    for h in range(4):
                        last = e.transpose(banks[ob][:, hs(h)], oT_sb[:, hs(h)], idf[:])
                    return last
                S.op("pe", tr_o, reads=[BoT, Bconst, Bc2, Bc3, Bidf], writes=[BK[ob]])
                yield

            for h in range(4):
                S.op("act", lambda e, h=h: e.activation(on[tb][:, hs(h)], banks[ob][:, hs(h)], AF.Square, accum_out=sso[tb][:, h:h + 1]),
                     reads=[BK[ob]], writes=[Bsso[tb], Bon[tb]])
            pow_rsqrt(rso[tb][:], sso[tb][:], Brso[tb], Bsso[tb], 1.0 / 128, 4)
            yield
            S.op("dve", lambda e: e.tensor_tensor(on[tb][:].rearrange("t (h v) -> t h v", h=4), banks[ob][:].rearrange("t (h v) -> t h v", h=4),
                                                  rso[tb][:].unsqueeze(2).to_broadcast([128, 4, 128]), ALU.mult),
                 reads=[BK[ob], Brso[tb]], writes=[Bon[tb]])
            if flag is not None:
                flag["done"] = True
            yield

            Wb = WbdTb if sample else WcTb
            Cc = C_S if sample else C_P

            def mm_m(e):
                last = None
                for h in range(4):
                    last = e.matmul(banks[3][:, hs(h)], vhat[vb][:, hs(h)], Wb[:, hs(h)], start=True, stop=True)
                return last
            S.op("pe", mm_m, reads=[Bvhat[vb], BW], writes=[BK[3]])
            for h in range(4):
                S.op("dve", lambda e, h=h: e.scalar_tensor_tensor(t1[tb][:, hs(h)], banks[3][:, hs(h)], gln[:, h:h + 1], Cc[:, hs(h)], ALU.mult, ALU.add),
                     reads=[BK[3], Bcols, Bc2, Bc3, Bc4, BW], writes=[Bt1[tb]])
            S.op("pool", lambda e: e.tensor_tensor(Pm[tb][:].rearrange("p (h t) -> p h t", h=4), guv[:, :, sl], szbv[:, :, sl], ALU.mult),
                 reads=Bgu[mb] + Bszb[mb], writes=[BPm[tb]])
            S.op("pool", lambda e: e.tensor_tensor(mixT[tb][:, 4:8, :].rearrange("p h t -> p (h t)"), t1[tb][:], Pm[tb][:], ALU.mult),
                 reads=[Bt1[tb], BPm[tb]], writes=[BmixB[tb]])
            yield

        def gen_Hb(n):
            sample = (n == NTILE_P)
            m, j = (n // 2, n % 2) if not sample else (NMT_P, 0)
            mb = midx(m) % 2
            NT = 128 if sample else 256
            tb = htb[n]
            szav = fm_view(sza[mb], NT)
            sl = slice(128 * j, 128 * j + 128)
            src, dsty = tile_rows(n)
            hs = lambda h: slice(128 * h, 128 * h + 128)
            pb5 = banks[7][:, 0:256].bitcast(BF16)

            def tr_on(e):
                last = None
                for h in range(4):
                    last = e.transpose(pb5[:, hs(h)], on[tb][:, hs(h)], idb[:])
                return last
            S.op("pe", tr_on, reads=[Bon[tb], Bidb], writes=[BK[7]])
            for h in range(4):
                S.op("dve", lambda e, h=h: e.scalar_tensor_tensor(mixT[tb][:, h, :], pb5[:, hs(h)], ghn[:, h:h + 1], szav[:, h, sl], ALU.mult, ALU.mult),
                     reads=[BK[7], Bcols, Bc2, Bc3, Bc4] + (Bsza[mb] if sample else [Bsza[mb][h // 2]]), writes=[BmixA[tb]])
            yield

            flush_stores()
            for hf in range(2):
                bko = 7 if hf == 0 else 3

                def mm_out(e, hf=hf, bko=bko):
                    last = None
                    for c in range(8):
                        last = e.matmul(banks[bko][:], mixT[tb][:, c, :], w_out_sb[:, c, 512 * hf:512 * hf + 512], start=(c == 0), stop=(c == 7))
                    return last
                S.op("pe", mm_out, reads=[BmixA[tb], BmixB[tb]] + Bw_out[hf], writes=[BK[bko]])
                S.op("dve", lambda e, hf=hf, bko=bko: e.tensor_tensor(res[tb][:, 512 * hf:512 * hf + 512], banks[bko][:], xres[tb][:, 512 * hf:512 * hf + 512], ALU.add),
                     reads=[BK[bko], Bxres[tb]], writes=[Bres[tb]])
                yield
            S.op("act", lambda e: e.activation(mixT[tb][:].rearrange("p c t -> p (c t)"), res[tb][:], AF.Square, accum_out=ssf[tb][:]), reads=[Bres[tb]], writes=[Bssf[tb], BmixA[tb], BmixB[tb]])
            pow_rsqrt(rsf[tb][:], ssf[tb][:], Brsf[tb], Bssf[tb], 1.0 / D, 1)
            yield
            S.op("act", lambda e: e.activation(res[tb][:], res[tb][:], AF.Copy, scale=rsf[tb][:]), reads=[Bres[tb], Brsf[tb]], writes=[Bres[tb]])
            S.op("pool", lambda e: e.tensor_tensor(res[tb][:], res[tb][:], gf_bc[:], ALU.mult), reads=[Bres[tb], Bconst, Bc2, Bc3], writes=[Bres[tb]])
            pend_st.append(lambda: out_events.append(S.dma(STQ, sem_y[tb], lambda e, sem: e.dma_start(out=dsty, in_=res[tb][:]).then_inc(sem, 16), reads=[Bres[tb]])))
            yield

        def run(g):
            for _ in g:
                pass

        def igen(*gens):
            alive = [g for g in gens if g is not None]
            while alive:
                nxt = []
                for g in alive:
                    try:
                        next(g)
                        nxt.append(g)
                        yield
                    except StopIteration:
                        pass
                alive = nxt

        RATE_H, RATE_X = _DBG.get("rate_h", 1.25), _DBG.get("rate_x", 1.0)
        MT_ORDER = [0, NMT_P] + list(range(1, NMT_P))
        Bmh = Buf("mhalf")
        Bngc = Buf("ngc")
        sem_c0 = S.dmasem("const0")
        S.op("pool", lambda e: e.memset(mhalf[:], -0.5), writes=[Bmh])
        S.op("pool", lambda e: e.memset(ones1[:], 1.0), writes=[Bmh])
        S.op("pool", lambda e: e.memset(zero_col[:], 0.0), writes=[Bmh])
        for i in range(3):
            S.op("pool", lambda e, i=i: e.memset(sTm[i][:], 0.0), writes=[BsTm[i]])
        g_x0 = [gen_xn(n, queue="act") for n in tiles_of(MT_ORDER[0])]
        Bidf = Buf("idf")
        Bc4 = Buf("cols_t")
        Bstg = Buf("colstg")
        sem_c5 = S.dmasem("const5")
        colstg = KhT[1][:].bitcast(F32)[0:24, 0:128]

        def ld_rows(e, sem):
            stg = KhT[1][:].bitcast(F32)
            e.dma_start(out=idf[:], in_=c_idf).then_inc(sem, 16)
            e.dma_start(out=stg[0:8, 0:128], in_=norm_g.rearrange("(c p) -> c p", p=128)).then_inc(sem, 16)
            e.dma_start(out=stg[8:16, 0:128], in_=lb_logits.rearrange("r (h k) -> (r h) k", k=128)).then_inc(sem, 16)
            e.dma_start(out=stg[16:20, 0:128], in_=hgrn_g.rearrange("(h v) -> h v", v=128)).then_inc(sem, 16)
            e.dma_start(out=stg[20:24, 0:128], in_=ln_g.rearrange("(h e) -> h e", e=128)).then_inc(sem, 16)
        S.dma("sp", sem_c5, ld_rows, writes=[Bstg, BKhT[1], Bidf], n=5)
        S.op("pe", lambda e: e.transpose(banks[6][:, 0:24], colstg, idf[0:24, 0:24]), reads=[Bstg, Bidf], writes=[BK[6]])
        S.op("dve", lambda e: e.tensor_copy(ngc[:], banks[6][:, 0:8]), reads=[BK[6]], writes=[Bngc])
        S.op("dve", lambda e: e.tensor_copy(lbl[:].rearrange("p r h -> p (r h)"), banks[6][:, 8:16]), reads=[BK[6]], writes=[Bcols])
        S.op("dve", lambda e: e.tensor_copy(ghn[:], banks[6][:, 16:20]), reads=[BK[6]], writes=[Bc4])
        S.op("dve", lambda e: e.tensor_copy(gln[:], banks[6][:, 20:24]), reads=[BK[6]], writes=[Bc4])
        next(g_x0[0])
        next(g_x0[1])
        Bidb = Buf("idb")
        sem_c4 = S.dmasem("const4")
        S.dma("act", sem_c4, lambda e, sem: e.dma_start(out=idb[:], in_=c_idb).then_inc(sem, 16), writes=[Bidb])
        next(g_x0[0])
        Bc2, Bc3 = Buf("const_late"), Buf("cols_late")
        sem_c3 = S.dmasem("const3")

        def ld_urgent(e, sem):
            e.dma_start(out=resetP[:], in_=c_resetP).then_inc(sem, 16)
        S.dma("act", sem_c3, ld_urgent, writes=[Bconst], n=1)
        def ld_cols(e, sem):
            def d(o, i):
                e.dma_start(out=o, in_=i, allow_slow_non_contiguous=True).then_inc(sem, 16)
            for h in range(4):
                d(Z8[:, h, :].rearrange("t (j s) -> t j s", j=16), w_s[h, 0:8, 0:8].unsqueeze(1).broadcast_to([8, 16, 8]))
            d(bsrowS.rearrange("o (h j t) -> o h j t", h=4, j=16), b_s[:, 0:8].unsqueeze(1).broadcast_to([4, 16, 8]).unsqueeze(0))
        S.dma("act", sem_c2, ld_cols, writes=[Bc3, Bres[1], BW], n=5)

        run(g_x0[0])
        next(g_x0[1])
        ensure_w(W_ORDER[0])
        run(g_x0[1])
        ensure_w(W_ORDER[1])
        S.op("dve", lambda e: e.tensor_tensor(lncol[:, :, 0], lbl[:, 0, :], lbl[:, 1, :], ALU.subtract), reads=[Bcols], writes=[Bcols])
        S.op("act", lambda e: e.activation(lncol[:, :, 1], lncol[:, :, 0], AF.Tanh, scale=0.5), reads=[Bcols], writes=[Bcols])
        S.op("dve", lambda e: e.tensor_scalar(lncol[:, :, 3], lncol[:, :, 1], -0.25, 0.25, ALU.mult, ALU.add), reads=[Bcols], writes=[Bcols])
        S.op("dve", lambda e: e.tensor_scalar(lncol[:, :, 2], lncol[:, :, 1], 0.25, -0.25, ALU.mult, ALU.add), reads=[Bcols], writes=[Bcols])
        S.op("dve", lambda e: e.tensor_scalar(lncol[:, :, 0], lncol[:, :, 1], -0.25, 0.25, ALU.mult, ALU.add), reads=[Bcols], writes=[Bcols])
        S.op("dve", lambda e: e.tensor_scalar(lncol[:, :, 1], lncol[:, :, 1], 0.25, 0.75, ALU.mult, ALU.add), reads=[Bcols], writes=[Bcols])

        ensure_w(W_ORDER[2])
        def late_consts():
            def ld_consts(e, sem):
                def d(o, i):
                    e.dma_start(out=o, in_=i).then_inc(sem, 16)
                d(maskP[:], c_maskP)
                d(maskS[:], c_maskS)
                d(BDf[:], c_BDf)
                d(resetS[:], c_resetS)
                d(R8[:], c_R8)
                d(mcol[:], c_mcol)
                d(gf_bc[:], fin_g.partition_broadcast(128))
                d(g_bc[:], ln_g.partition_broadcast(128))
                d(b_bc[:], ln_b.partition_broadcast(128))
                d(bsrow, b_s.rearrange("h t -> (h t)").unsqueeze(0))
                d(maskPf_sc, c_maskPf)
                d(res[0][:, 0:512].rearrange("p (h s) -> p h s", h=4), w_s.rearrange("h t s -> t h s"))
            NCONST = 12
            S.dma("sp", sem_c, ld_consts, writes=[Bc2, Bres[0], Bres[1], BS, BKhT[0]], n=NCONST)


        def setup_wprep():
            Wt = res[0][:, 0:512]
            WcT32 = res[0][:, 512:1024]
            WbdT32 = res[1][:, 0:512]
            maskPf = maskPf_sc
            beta_bc = b_bc

            def w_tr(e):
                last = None
                for h in range(4):
                    last = e.transpose(banks[3][:, 128 * h:128 * h + 128], Wt[:, 128 * h:128 * h + 128], idf[:])
                return last
            S.op("pe", w_tr, reads=[Bconst, Bc2, Bc3, Bidf, Bres[0]], writes=[BK[3]])
            for h in range(4):
                S.op("dve", lambda e, h=h: e.tensor_tensor(WcT32[:, 128 * h:128 * h + 128], banks[3][:, 128 * h:128 * h + 128], maskPf, ALU.mult),
                     reads=[BK[3], BKhT[0]], writes=[Bres[0]])
            S.op("dve", lambda e: e.tensor_copy(WcTb[:], WcT32), reads=[Bres[0]], writes=[BW])

            def w_tile(e):
                last = None
                for h in range(4):
                    last = e.matmul(banks[4][:, 128 * h:128 * h + 128], Z8[:, h, :], R8[:], start=True, stop=True)
                return last
            S.op("pe", w_tile, reads=[Bcols, Bconst, Bc2, Bc3, BW], writes=[BK[4]])
            for h in range(4):
                S.op("dve", lambda e, h=h: e.tensor_tensor(WbdT32[:, 128 * h:128 * h + 128], banks[4][:, 128 * h:128 * h + 128], BDf[:], ALU.mult),
                     reads=[BK[4], Bconst, Bc2, Bc3], writes=[Bres[1]])
            S.op("dve", lambda e: e.tensor_copy(WbdTb[:], WbdT32), reads=[Bres[1]], writes=[BW])

            def c_mm(e, Wsrc, brow, bank):
                last = None
                for h in range(4):
                    o = bank[:, 128 * h:128 * h + 128]
                    e.matmul(o, beta_bc[:, 128 * h:128 * h + 128], Wsrc[:, 128 * h:128 * h + 128], start=True, stop=False)
                    last = e.matmul(o, ones1[:], brow[:, 128 * h:128 * h + 128], start=False, stop=True)
                return last
            S.op("pe", lambda e: c_mm(e, WcT32, bsrow, banks[5]), reads=[Bres[0], Bconst, Bc2, Bc3, BS, Bmh], writes=[BK[5]])
            S.op("act", lambda e: e.activation(C_P[:], banks[5][:], AF.Copy), reads=[BK[5]], writes=[BW])
            S.op("pe", lambda e: c_mm(e, WbdT32, bsrowS, banks[6]), reads=[Bres[1], Bconst, Bc2, Bc3, Bmh], writes=[BK[6]])
            S.op("act", lambda e: e.activation(C_S[:], banks[6][:], AF.Copy), reads=[BK[6]], writes=[BW])
            S.op("pool", lambda e: e.memset(Sst[:], 0.0), writes=[BS])


        def flagged(g, flag):
            for _ in g:
                yield
            flag["done"] = True

        def delayed(g, k):
            for _ in range(k):
                yield
            for _ in g:
                yield

        def after(flag, g):
            while not flag.get("done"):
                yield
            for _ in g:
                yield

        prev_hb = [None]

        def h_round(tiles, final=False):
            gens = []
            if prev_hb[0] is not None:
                gens.append(gen_Hb(prev_hb[0]))
            flags = [dict() for _ in tiles]
            for k, n in enumerate(tiles):
                g = flagged(gen_Hf(n, flags[k]), flags[k])
                gens.append(delayed(g, 2 * k) if k else g)
            for k, n in enumerate(tiles if final else tiles[:-1]):
                gens.append(after(flags[k], gen_Hb(n)))
            prev_hb[0] = None if final else tiles[-1]
            return igen(*gens)

        for r, m in enumerate(MT_ORDER):
            nxt_tiles = tiles_of(MT_ORDER[r + 1]) if r + 1 < len(MT_ORDER) else []
            gX = chain(*[gen_xn(n) for n in nxt_tiles]) if nxt_tiles else None
            gH = h_round(tiles_of(MT_ORDER[r - 1])) if r >= 1 else None
            interleave_w([(gen_proj(m, first=(r == 0)), 1.0), (gH, RATE_H), (gX, RATE_X)])
            if r == 0:
                late_consts()
                setup_wprep()
                ensure_w("out")
        run(h_round(tiles_of(MT_ORDER[-1]), final=True))
        flush_stores()
        S.final_wait(STQ, out_events)
        S.emit_all()
    return nc


_CACHE = {}
_DBG = {}


def _consts():
    p = np.arange(128)
    tri = (p[:, None] <= p[None, :])
    same = (p[:, None] // 8 == p[None, :] // 8) & (p[:, None] % 8 <= p[None, :] % 8)
    c = {}
    c["c_idb"] = np.eye(128, dtype=np.float32).astype(ml_dtypes.bfloat16)
    c["c_idf"] = np.eye(128, dtype=np.float32)
    c["c_maskP"] = np.ascontiguousarray(np.tile(tri.astype(np.uint8), (1, 4)))
    c["c_maskS"] = np.ascontiguousarray(np.tile(same.astype(np.uint8), (1, 4)))
    c["c_maskPf"] = tri.astype(np.float32)
    c["c_BDf"] = same.astype(np.float32)
    rp = np.ones((128, 1024), np.float32)
    rp[:, ::128] = 0
    c["c_resetP"] = rp.astype(ml_dtypes.bfloat16)
    rs = np.ones((128, 128), np.float32)
    rs[:, ::8] = 0
    c["c_resetS"] = rs.astype(ml_dtypes.bfloat16)
    c["c_R8"] = (np.arange(8)[:, None] == (p[None, :] % 8)).astype(np.float32)
    c["c_mcol"] = (p[:, None] // 8 == np.arange(16)[None, :]).astype(np.float32)
    return c


def kernel(x_prompt, x_sample, state_hgrn, norm_g, w_in, lb_logits, hgrn_norm_g,
           sgu_ln_g, sgu_ln_b, w_s, b_s, w_out, final_norm_g):
    f32 = lambda a: np.ascontiguousarray(np.asarray(a, dtype=np.float32))
    x_prompt, x_sample, state_hgrn = f32(x_prompt), f32(x_sample), f32(state_hgrn)
    if "nc" not in _CACHE:
        _CACHE["nc"] = build_nc()
    nc = _CACHE["nc"]
    consts = _consts()
    shared = {
        "norm_g": f32(norm_g).reshape(D), "w_in": f32(w_in).reshape(D, DIN), "lb_logits": f32(lb_logits).reshape(2, 512),
        "hgrn_g": f32(hgrn_norm_g).reshape(512), "ln_g": f32(sgu_ln_g).reshape(512), "ln_b": f32(sgu_ln_b).reshape(512),
        "w_s": f32(w_s).reshape(4, 128, 128), "b_s": f32(b_s).reshape(4, 128), "w_out": f32(w_out).reshape(D, D),
        "fin_g": f32(final_norm_g).reshape(D),
    }
    shared.update(consts)
    in_maps = []
    for b in range(NCORES):
        mp = dict(shared)
        mp["xp"] = x_prompt[b]
        mp["xs"] = x_sample[16 * b:16 * b + 16].reshape(128, D)
        mp["s0"] = state_hgrn[0, 16 * b:16 * b + 16]
        in_maps.append(mp)
    resr = run_bass_kernel_spmd(nc, in_maps, core_ids=list(range(NCORES)))
    r = resr.results
    g = lambda k, b: np.asarray(r[b][k], dtype=np.float32)
    y_prompt = np.stack([g("yp", b) for b in range(NCORES)], 0)
    y_sample = np.concatenate([g("ys", b).reshape(16, 8, D) for b in range(NCORES)], 0)
    st_p = np.stack([g("sp_o", b) for b in range(NCORES)], 0)[None]
    st_s = np.concatenate([g("ss_o", b) for b in range(NCORES)], 0)[None]
    v_p = np.stack([g("vp_o", b).reshape(128, 4, 128) for b in range(NCORES)], 0)[None]
    v_s = np.concatenate([g("vs_o", b).reshape(16, 8, 4, 128) for b in range(NCORES)], 0)[None]
    return (y_prompt, y_sample, st_p, st_s, v_p, v_s)
```
